# Optimizing a Trainium2 kernel written in Bass

```python
import math
import jax, jax.numpy as jnp
from jax import lax
import numpy as np

D_MODEL = 2048
BATCH = 16
SEQ = 256
DEPTH = 4
DEC_BATCH = 2
DEC_SEQ = 2048
PAST_LEN = 256

GRID_W = 64
N_HEADS = 4
HEAD_DIM = 128
MIX_W = N_HEADS * HEAD_DIM
N_KV = 2
KV_W = N_KV * HEAD_DIM
GQA_GROUP = N_HEADS // N_KV
GLA_RANK = 16
GLA_TAU = 16.0
CHUNK = 64
Q_BLOCK = 128
CONV_K = 5
D_FF = 5632
ROPE_THETA = 10000.0
N_BRANCH = 4
N_DIR = 2
N_MOD = 9
EPS = 1e-6

_COLS = (
    ('gla_q', MIX_W), ('gla_k', MIX_W), ('gla_v', MIX_W), ('gla_g', MIX_W),
    ('gla_lr', N_DIR * GLA_RANK),
    ('ml_q', MIX_W), ('ml_k', MIX_W), ('ml_v', MIX_W), ('ml_o', MIX_W),
    ('ml_if', N_DIR * 2 * N_HEADS),
    ('gd_qkv', 3 * MIX_W), ('gd_g', MIX_W), ('gd_ab', N_DIR * 2 * N_HEADS),
    ('at_q', MIX_W), ('at_k', KV_W), ('at_v', KV_W),
    ('merge', N_BRANCH * D_MODEL),
)
IN_W = (4 * MIX_W + N_DIR * GLA_RANK + 4 * MIX_W + N_DIR * 2 * N_HEADS
        + 4 * MIX_W + N_DIR * 2 * N_HEADS + MIX_W + 2 * KV_W + N_BRANCH * D_MODEL)

kernel_name = 'hybrid_diffusion_prefix_trunk_step'


def _rms(x, w=None):
    xf = x.astype(jnp.float32)
    y = xf * lax.rsqrt(jnp.mean(xf * xf, axis=-1, keepdims=True) + EPS)
    if w is not None:
        y = y * w.astype(jnp.float32)
    return y.astype(x.dtype)


def _l2n(x):
    xf = x.astype(jnp.float32)
    return xf * lax.rsqrt(jnp.sum(xf * xf, axis=-1, keepdims=True) + EPS)


def _split_cols(z):
    out, off = {}, 0
    for name, width in _COLS:
        out[name] = z[..., off:off + width]
        off += width
    return out


def _heads(x, n=N_HEADS):
    return x.reshape(x.shape[:-1] + (n, HEAD_DIM))


def _to_chunks(x):
    B, T = x.shape[:2]
    x = x.reshape((B, T // CHUNK, CHUNK) + x.shape[2:])
    return jnp.moveaxis(jnp.moveaxis(x, 1, 0), 3, 2)


def _from_chunks(y):
    N, B, H, C = y.shape[:4]
    y = jnp.swapaxes(jnp.moveaxis(y, 0, 1), 2, 3)
    return y.reshape((B, N * C, H) + y.shape[4:])


def _masks():
    i = jnp.arange(CHUNK)
    return i[:, None] >= i[None, :], i[:, None] > i[None, :]


def _gla_step(S, xs):
    q, k, v, lg = xs
    causal, _ = _masks()
    b = jnp.cumsum(lg, axis=2)
    diff = b[:, :, :, None, :] - b[:, :, None, :, :]
    w = jnp.exp(jnp.where(causal[:, :, None], diff, -jnp.inf))
    att = jnp.einsum('bhtd,bhsd,bhtsd->bhts', q, k, w)
    o = jnp.einsum('bhtd,bhde->bhte', q * jnp.exp(b), S) + jnp.einsum('bhts,bhse->bhte', att, v)
    b_end = b[:, :, -1:, :]
    S_new = (jnp.exp(b_end)[:, :, 0, :, None] * S
             + jnp.einsum('bhsd,bhse->bhde', k * jnp.exp(b_end - b), v))
    return S_new, o


def _mlstm_step(carry, xs):
    Cs, ns, m = carry
    q, k, v, ig, lf = xs
    causal, _ = _masks()
    F = jnp.cumsum(lf, axis=-1)
    logD = jnp.where(causal, F[..., :, None] - F[..., None, :] + ig[..., None, :], -jnp.inf)
    inter = F + m[..., None]
    m_t = jnp.maximum(inter, jnp.max(logD, axis=-1))
    D = jnp.exp(logD - m_t[..., None])
    a_in = jnp.exp(inter - m_t)
    s = jnp.einsum('bhtd,bhsd->bhts', q, k) * D
    num = a_in[..., None] * jnp.einsum('bhtd,bhde->bhte', q, Cs) + jnp.einsum('bhts,bhse->bhte', s, v)
    den = a_in * jnp.einsum('bhtd,bhd->bht', q, ns) + jnp.sum(s, axis=-1)
    h = num / jnp.maximum(jnp.abs(den), jnp.exp(-m_t))[..., None]
    m_new = m_t[..., -1]
    w_end = jnp.exp(F[..., -1:] - F + ig - m_new[..., None])
    a_end = jnp.exp(F[..., -1] + m - m_new)
    kw = k * w_end[..., None]
    C_new = a_end[..., None, None] * Cs + jnp.einsum('bhsd,bhse->bhde', kw, v)
    n_new = a_end[..., None] * ns + jnp.sum(kw, axis=2)
    return (C_new, n_new, m_new), h


def _gdn_step(S, xs):
    q, k, v, g, beta = xs
    causal, strict = _masks()
    gam = jnp.cumsum(g, axis=-1)
    decay = jnp.exp(jnp.where(causal, gam[..., :, None] - gam[..., None, :], -jnp.inf))
    kk = jnp.einsum('bhtd,bhsd->bhts', k, k)
    tri = jnp.eye(CHUNK, dtype=kk.dtype) + jnp.where(strict, beta[..., None] * kk * decay, 0.0)
    rhs = jnp.concatenate([v * beta[..., None], k * (beta * jnp.exp(gam))[..., None]], axis=-1)
    sol = lax.linalg.triangular_solve(tri, rhs, left_side=True, lower=True, unit_diagonal=True)
    dv = v.shape[-1]
    w_new = sol[..., :dv] - jnp.einsum('bhtd,bhde->bhte', sol[..., dv:], S)
    qk = jnp.einsum('bhtd,bhsd->bhts', q, k) * decay
    o = (jnp.einsum('bhtd,bhde->bhte', q * jnp.exp(gam)[..., None], S)
         + jnp.einsum('bhts,bhse->bhte', qk, w_new))
    g_end = gam[..., -1:]
    S_new = (jnp.exp(g_end)[..., None] * S
             + jnp.einsum('bhsd,bhse->bhde', k * jnp.exp(g_end - gam)[..., None], w_new))
    return S_new, o


def _run_dir(step, xs_tok, s0, reverse):
    if reverse:
        xs_tok = tuple(jnp.flip(a, axis=1) for a in xs_tok)
    s_fin, ys = lax.scan(step, s0, tuple(_to_chunks(a) for a in xs_tok))
    y = _from_chunks(ys)
    if reverse:
        y = jnp.flip(y, axis=1)
    return y, s_fin


def _short_conv(x, w):
    pad = CONV_K // 2
    return lax.conv_general_dilated(x, w[:, None, :].astype(x.dtype), window_strides=(1,),
                                    padding=[(pad, pad)], dimension_numbers=('NWC', 'WIO', 'NWC'),
                                    feature_group_count=x.shape[-1])


def _rope_2d(x):
    T = x.shape[1]
    rows = T // GRID_W
    row = jnp.repeat(jnp.arange(rows), GRID_W).astype(jnp.float32)
    col = (jnp.arange(rows * GRID_W) % GRID_W).astype(jnp.float32)
    n_pairs = HEAD_DIM // 4
    inv = ROPE_THETA ** (-jnp.arange(n_pairs, dtype=jnp.float32) / n_pairs)
    ang = jnp.concatenate([row[:, None] * inv, col[:, None] * inv], axis=-1)
    cos = jnp.cos(ang)[None, :, None, :]
    sin = jnp.sin(ang)[None, :, None, :]
    xf = x.astype(jnp.float32).reshape(x.shape[:-1] + (HEAD_DIM // 2, 2))
    x1, x2 = xf[..., 0], xf[..., 1]
    out = jnp.stack([x1 * cos - x2 * sin, x1 * sin + x2 * cos], axis=-1).reshape(x.shape)
    return out.astype(x.dtype)


def _attend(q, k, v):
    B, T = q.shape[:2]
    nb = T // Q_BLOCK
    qb = jnp.moveaxis(q.reshape((B, nb, Q_BLOCK) + q.shape[2:]), 1, 0)
    scale = HEAD_DIM ** -0.5

    def one(qblk):
        s = jnp.einsum('bqkgd,bskd->bkgqs', qblk, k).astype(jnp.float32) * scale
        p = jax.nn.softmax(s, axis=-1).astype(v.dtype)
        return jnp.einsum('bkgqs,bskd->bqkgd', p, v)

    o = lax.map(one, qb)
    return jnp.moveaxis(o, 0, 1).reshape(q.shape)


def _swiglu(h, w_in, w_out):
    g, u = jnp.split(h @ w_in, 2, axis=-1)
    return (jax.nn.silu(g) * u) @ w_out


def _mixer(h, lp, st):
    f32 = jnp.float32
    B, T, _ = h.shape
    z = _split_cols(h @ lp['w_in'])
    if st is None:
        s_gla = jnp.zeros((B, N_DIR, N_HEADS, HEAD_DIM, HEAD_DIM), f32)
        s_mc = jnp.zeros((B, N_DIR, N_HEADS, HEAD_DIM, HEAD_DIM), f32)
        s_mn = jnp.zeros((B, N_DIR, N_HEADS, HEAD_DIM), f32)
        s_mm = jnp.zeros((B, N_DIR, N_HEADS), f32)
        s_gd = jnp.zeros((B, N_DIR, N_HEADS, HEAD_DIM, HEAD_DIM), f32)
    else:
        s_gla, s_mc, s_mn, s_mm, s_gd = (st[n].astype(f32) for n in ('gla', 'mc', 'mn', 'mm', 'gd'))

    gq = _heads(z['gla_q']).astype(f32) * HEAD_DIM ** -0.5
    gk = _heads(z['gla_k']).astype(f32)
    gv = _heads(z['gla_v']).astype(f32)
    lr = z['gla_lr'].reshape(B, T, N_DIR, GLA_RANK)
    glog = jax.nn.log_sigmoid((jnp.einsum('btir,irk->btik', lr, lp['gla_w2'])
                               + lp['gla_b2']).astype(f32)) / GLA_TAU
    glog = glog.reshape(B, T, N_DIR, N_HEADS, HEAD_DIM)
    gla_out, gla_fin = [], []
    for d in range(N_DIR):
        y, s = _run_dir(_gla_step, (gq, gk, gv, glog[:, :, d]), s_gla[:, d], d == 1)
        gla_out.append(y)
        gla_fin.append(s)
    y_gla = _rms(gla_out[0] + gla_out[1], lp['gla_norm_w']) * jax.nn.silu(_heads(z['gla_g']).astype(f32))

    mq = _heads(z['ml_q']).astype(f32)
    mk = _heads(z['ml_k']).astype(f32) * HEAD_DIM ** -0.5
    mv = _heads(z['ml_v']).astype(f32)
    gates = z['ml_if'].reshape(B, T, N_DIR, 2, N_HEADS).astype(f32) + lp['ml_gate_b'].astype(f32)
    ml_out, ml_fin = [], []
    for d in range(N_DIR):
        ig = gates[:, :, d, 0]
        lf = jax.nn.log_sigmoid(gates[:, :, d, 1])
        y, s = _run_dir(_mlstm_step, (mq, mk, mv, ig, lf), (s_mc[:, d], s_mn[:, d], s_mm[:, d]), d == 1)
        ml_out.append(y)
        ml_fin.append(s)
    y_ml = _rms(ml_out[0] + ml_out[1], lp['ml_norm_w']) * jax.nn.sigmoid(_heads(z['ml_o']).astype(f32))

    qkv = jax.nn.silu(_short_conv(z['gd_qkv'], lp['gd_conv_w']))
    dq, dk, dv = jnp.split(qkv, 3, axis=-1)
    dq = _l2n(_heads(dq)) * HEAD_DIM ** -0.5
    dk = _l2n(_heads(dk))
    dv = _heads(dv).astype(f32)
    ab = z['gd_ab'].reshape(B, T, N_DIR, 2, N_HEADS).astype(f32)
    gd_out, gd_fin = [], []
    for d in range(N_DIR):
        g = -jnp.exp(lp['gd_a_log'][d].astype(f32)) * jax.nn.softplus(ab[:, :, d, 0] + lp['gd_dt_bias'][d].astype(f32))
        beta = jax.nn.sigmoid(ab[:, :, d, 1])
        y, s = _run_dir(_gdn_step, (dq, dk, dv, g, beta), s_gd[:, d], d == 1)
        gd_out.append(y)
        gd_fin.append(s)
    y_gd = _rms(gd_out[0] + gd_out[1], lp['gd_norm_w']) * jax.nn.silu(_heads(z['gd_g']).astype(f32))

    aq = _rms(_heads(z['at_q']), lp['q_norm_w'])
    ak = _rms(_heads(z['at_k'], N_KV), lp['k_norm_w'])
    av = _heads(z['at_v'], N_KV)
    if st is None:
        k_all, v_all = ak, av
    else:
        aq = _rope_2d(aq)
        k_all = jnp.concatenate([st['k'].astype(ak.dtype), _rope_2d(ak)], axis=1)
        v_all = jnp.concatenate([st['v'].astype(av.dtype), av], axis=1)
    y_at = _attend(aq.reshape(B, T, N_KV, GQA_GROUP, HEAD_DIM), k_all, v_all)

    ybr = jnp.stack([y_gla.reshape(B, T, MIX_W).astype(h.dtype), y_ml.reshape(B, T, MIX_W).astype(h.dtype),
                     y_gd.reshape(B, T, MIX_W).astype(h.dtype), y_at.reshape(B, T, MIX_W).astype(h.dtype)], axis=2)
    mg = jax.nn.sigmoid(z['merge'].reshape(B, T, N_BRANCH, D_MODEL).astype(f32)).astype(h.dtype)
    merged = jnp.sum(mg * jnp.einsum('btnc,ncd->btnd', ybr, lp['w_branch']), axis=2)
    out = merged @ lp['w_out']
    if st is None:
        ctx = dict(k=ak, v=av, gla=jnp.stack(gla_fin, axis=1),
                   mc=jnp.stack([s[0] for s in ml_fin], axis=1),
                   mn=jnp.stack([s[1] for s in ml_fin], axis=1),
                   mm=jnp.stack([s[2] for s in ml_fin], axis=1),
                   gd=jnp.stack(gd_fin, axis=1))
    else:
        ctx = None
    return out, ctx


def _layer(x, cvec, lp, st):
    mod = jax.nn.silu(cvec) @ lp['w_ada'] + lp['b_ada']
    sh1, sc1, g1, sh2, sc2, g2, sh3, sc3, g3 = jnp.split(mod[:, None, :], N_MOD, axis=-1)
    x = x + 0.5 * g1 * _swiglu(_rms(x) * (1 + sc1) + sh1, lp['w_ffn_in'][0], lp['w_ffn_out'][0])
    mix, ctx = _mixer(_rms(x) * (1 + sc2) + sh2, lp, st)
    x = x + g2 * mix
    x = x + 0.5 * g3 * _swiglu(_rms(x) * (1 + sc3) + sh3, lp['w_ffn_in'][1], lp['w_ffn_out'][1])
    return x, ctx


def setup_inputs(seed: int = 0) -> dict:
    key = jax.random.key(seed)
    ks = jax.random.split(key, 32)
    f32 = jnp.float32

    def nrm(k, shape, s):
        return jax.random.normal(k, shape, f32) * s

    st6 = (DEC_BATCH, DEPTH, N_DIR, N_HEADS, HEAD_DIM, HEAD_DIM)
    dt = jnp.exp(jax.random.uniform(ks[23], (DEPTH, N_DIR, N_HEADS), f32, math.log(1e-3), math.log(1e-1)))
    return {
        'x_prompt': nrm(ks[0], (BATCH, SEQ, D_MODEL), 1.0),
        'x_sample': nrm(ks[1], (DEC_BATCH, DEC_SEQ, D_MODEL), 1.0),
        'cache_k': nrm(ks[2], (DEC_BATCH, DEPTH, PAST_LEN, N_KV, HEAD_DIM), 1.0),
        'cache_v': nrm(ks[3], (DEC_BATCH, DEPTH, PAST_LEN, N_KV, HEAD_DIM), 1.0),
        'state_gla': nrm(ks[4], st6, 0.5),
        'state_mlstm_c': nrm(ks[5], st6, 0.5),
        'state_mlstm_n': nrm(ks[6], (DEC_BATCH, DEPTH, N_DIR, N_HEADS, HEAD_DIM), 0.5),
        'state_mlstm_m': nrm(ks[7], (DEC_BATCH, DEPTH, N_DIR, N_HEADS), 1.0),
        'state_gdn': nrm(ks[8], st6, 0.1),
        'c': nrm(ks[9], (DEC_BATCH, D_MODEL), 1.0),
        'c_ctx': nrm(ks[10], (D_MODEL,), 1.0),
        'w_ada': nrm(ks[11], (DEPTH, D_MODEL, N_MOD * D_MODEL), 0.5 * D_MODEL ** -0.5),
        'b_ada': nrm(ks[12], (DEPTH, N_MOD * D_MODEL), 0.02),
        'w_ffn_in': nrm(ks[13], (DEPTH, 2, D_MODEL, 2 * D_FF), D_MODEL ** -0.5),
        'w_ffn_out': nrm(ks[14], (DEPTH, 2, D_FF, D_MODEL), D_FF ** -0.5),
        'w_in': nrm(ks[15], (DEPTH, D_MODEL, IN_W), D_MODEL ** -0.5),
        'gla_w2': nrm(ks[16], (DEPTH, N_DIR, GLA_RANK, MIX_W), GLA_RANK ** -0.5),
        'gla_b2': nrm(ks[17], (DEPTH, N_DIR, MIX_W), 0.1),
        'gla_norm_w': 1.0 + nrm(ks[18], (DEPTH, HEAD_DIM), 0.1),
        'ml_gate_b': jnp.concatenate([nrm(ks[19], (DEPTH, N_DIR, 1, N_HEADS), 0.1),
                                      3.0 + nrm(ks[20], (DEPTH, N_DIR, 1, N_HEADS), 0.5)], axis=2),
        'ml_norm_w': 1.0 + nrm(ks[21], (DEPTH, HEAD_DIM), 0.1),
        'gd_conv_w': nrm(ks[22], (DEPTH, CONV_K, 3 * MIX_W), CONV_K ** -0.5),
        'gd_a_log': jnp.log(jax.random.uniform(ks[24], (DEPTH, N_DIR, N_HEADS), f32, 1.0, 16.0)),
        'gd_dt_bias': dt + jnp.log(-jnp.expm1(-dt)),
        'gd_norm_w': 1.0 + nrm(ks[25], (DEPTH, HEAD_DIM), 0.1),
        'q_norm_w': 1.0 + nrm(ks[26], (DEPTH, HEAD_DIM), 0.1),
        'k_norm_w': 1.0 + nrm(ks[27], (DEPTH, HEAD_DIM), 0.1),
        'w_branch': nrm(ks[28], (DEPTH, N_BRANCH, MIX_W, D_MODEL), MIX_W ** -0.5),
        'w_out': nrm(ks[29], (DEPTH, D_MODEL, D_MODEL), D_MODEL ** -0.5),
        'final_norm_w': 1.0 + nrm(ks[30], (D_MODEL,), 0.1),
    }


def reference(x_prompt, x_sample, cache_k, cache_v, state_gla, state_mlstm_c, state_mlstm_n,
              state_mlstm_m, state_gdn, c, c_ctx, w_ada, b_ada, w_ffn_in, w_ffn_out, w_in,
              gla_w2, gla_b2, gla_norm_w, ml_gate_b, ml_norm_w, gd_conv_w, gd_a_log, gd_dt_bias,
              gd_norm_w, q_norm_w, k_norm_w, w_branch, w_out, final_norm_w):
    layers = [dict(w_ada=w_ada[l], b_ada=b_ada[l], w_ffn_in=w_ffn_in[l], w_ffn_out=w_ffn_out[l],
                   w_in=w_in[l], gla_w2=gla_w2[l], gla_b2=gla_b2[l], gla_norm_w=gla_norm_w[l],
                   ml_gate_b=ml_gate_b[l], ml_norm_w=ml_norm_w[l], gd_conv_w=gd_conv_w[l],
                   gd_a_log=gd_a_log[l], gd_dt_bias=gd_dt_bias[l], gd_norm_w=gd_norm_w[l],
                   q_norm_w=q_norm_w[l], k_norm_w=k_norm_w[l], w_branch=w_branch[l], w_out=w_out[l])
              for l in range(DEPTH)]

    xp = x_prompt
    ctxs = []
    for l in range(DEPTH):
        xp, ctx = _layer(xp, c_ctx[None, :], layers[l], None)
        ctxs.append(ctx)
    y_prompt = _rms(xp, final_norm_w)

    xs = x_sample
    for l in range(DEPTH):
        st = dict(k=cache_k[:, l], v=cache_v[:, l], gla=state_gla[:, l], mc=state_mlstm_c[:, l],
                  mn=state_mlstm_n[:, l], mm=state_mlstm_m[:, l], gd=state_gdn[:, l])
        xs, _ = _layer(xs, c, layers[l], st)
    y_sample = _rms(xs, final_norm_w)

    new_cache_k = jnp.stack([cx['k'] for cx in ctxs], axis=1)
    new_cache_v = jnp.stack([cx['v'] for cx in ctxs], axis=1)
    new_state_gla = jnp.stack([cx['gla'] for cx in ctxs], axis=1)
    new_state_mlstm_c = jnp.stack([cx['mc'] for cx in ctxs], axis=1)
    new_state_mlstm_n = jnp.stack([cx['mn'] for cx in ctxs], axis=1)
    new_state_mlstm_m = jnp.stack([cx['mm'] for cx in ctxs], axis=1)
    new_state_gdn = jnp.stack([cx['gd'] for cx in ctxs], axis=1)
    return (y_prompt, y_sample, new_cache_k, new_cache_v, new_state_gla, new_state_mlstm_c,
            new_state_mlstm_n, new_state_mlstm_m, new_state_gdn)
```

```python
import contextlib
import math
import numpy as np
import concourse.bass as bass
import concourse.mybir as mybir
from concourse.bass_utils import run_bass_kernel_spmd

F32 = mybir.dt.float32
BF16 = mybir.dt.bfloat16
AF = mybir.ActivationFunctionType
ALU = mybir.AluOpType

D = 2048
KC = 16
DFF = 5632
HC = 44
INW = 15424
EPS = 1e-6
HD = 128
C_GLA_Q, C_GLA_K, C_GLA_V, C_GLA_G, C_GLA_LR = 0, 512, 1024, 1536, 2048
C_ML_Q, C_ML_K, C_ML_V, C_ML_O, C_ML_IF = 2080, 2592, 3104, 3616, 4128
C_GD_QKV, C_GD_G, C_GD_AB = 4144, 5680, 6192
C_AT_Q, C_AT_K, C_AT_V, C_MERGE = 6208, 6720, 6976, 7232


class Tok:
    __slots__ = ("name", "last_w", "readers", "psum")

    def __init__(self, name="", psum=False):
        self.name = name
        self.last_w = None
        self.readers = []
        self.psum = psum


class Op:
    __slots__ = ("eng", "fn", "deps", "needed", "val", "dma_sem", "is_dma")

    def __init__(self, eng, fn, is_dma=False):
        self.eng = eng
        self.fn = fn
        self.deps = []
        self.needed = False
        self.val = None
        self.dma_sem = None
        self.is_dma = is_dma


class Prog:
    ENGS = ("pe", "act", "dve", "pool", "sp")

    def __init__(self, nc, n_dma_sems=48):
        self.nc = nc
        self.ops = {e: [] for e in self.ENGS}
        self.n_dma_sems = n_dma_sems
        self.dma_rr = 0
        self.dma_rr_sw = 0
        self.dma_last = [None] * n_dma_sems
        self.dma_cnt = [0] * n_dma_sems
        self.all_ops = []

    def _deps(self, op, reads, writes):
        for t in reads:
            if t.last_w is not None:
                op.deps.append(t.last_w)
            if t.psum:
                assert all(r.eng == op.eng for r in t.readers), "two engines read PSUM tile " + t.name
        for t in writes:
            if t.last_w is not None:
                op.deps.append(t.last_w)
            op.deps.extend(t.readers)
        for t in reads:
            t.readers.append(op)
        for t in writes:
            t.last_w = op
            t.readers = []

    def op(self, eng, fn, reads=(), writes=()):
        o = Op(eng, fn)
        self._deps(o, reads, writes)
        self.ops[eng].append(o)
        self.all_ops.append(o)
        return o

    def dma(self, eng, fn, reads=(), writes=()):
        o = Op(eng, fn, is_dma=True)
        if eng == "pool":
            j = self.dma_rr_sw
            self.dma_rr_sw = (self.dma_rr_sw + 1) % 12
        else:
            j = 12 + self.dma_rr
            self.dma_rr = (self.dma_rr + 1) % (self.n_dma_sems - 12)
        o.dma_sem = j
        if self.dma_last[j] is not None:
            o.deps.append(self.dma_last[j])
        self.dma_cnt[j] += 1
        o.val = 16 * self.dma_cnt[j]
        self.dma_last[j] = o
        o.needed = True
        self._deps(o, reads, writes)
        self.ops[eng].append(o)
        self.all_ops.append(o)
        return o

    def barrier(self):
        last = [self.ops[e][-1] for e in self.ENGS if self.ops[e]]
        last += [d for d in self.dma_last if d is not None]
        for e in self.ENGS:
            o = Op(e, lambda h: h.nop())
            o.deps = list(last)
            self.ops[e].append(o)
            self.all_ops.append(o)

    def emit(self):
        nc = self.nc
        for o in self.all_ops:
            for d in o.deps:
                d.needed = True
        for e in self.ENGS:
            c = 0
            for o in self.ops[e]:
                if o.is_dma:
                    continue
                if o.needed:
                    c += 1
                    o.val = c
        with contextlib.ExitStack() as st:
            esem = {e: st.enter_context(nc.semaphore("s_" + e)) for e in self.ENGS}
            dsem = [st.enter_context(nc.semaphore("d%d" % j)) for j in range(self.n_dma_sems)]
            block = st.enter_context(nc.Block())

            def run(ename, h):
                known = {}
                for o in self.ops[ename]:
                    need = {}
                    for d in o.deps:
                        k = ("d", d.dma_sem) if d.is_dma else ("e", d.eng)
                        if (not d.is_dma) and d.eng == ename and ename == "pe":
                            continue
                        if known.get(k, 0) >= d.val:
                            continue
                        if need.get(k, 0) < d.val:
                            need[k] = d.val
                    for k, v in need.items():
                        s = dsem[k[1]] if k[0] == "d" else esem[k[1]]
                        h.wait_ge(s, v)
                        known[k] = v
                    ins = o.fn(h)
                    if o.is_dma:
                        ins.then_inc(dsem[o.dma_sem], 16)
                    elif o.needed:
                        ins.then_inc(esem[ename], 1)
                last = {}
                for o in self.ops[ename]:
                    if o.is_dma:
                        last[o.dma_sem] = max(last.get(o.dma_sem, 0), o.val)
                for j, v in last.items():
                    if known.get(("d", j), 0) < v:
                        h.wait_ge(dsem[j], v)

            @block.tensor
            def _(h):
                run("pe", h)

            @block.scalar
            def _(h):
                run("act", h)

            @block.vector
            def _(h):
                run("dve", h)

            @block.gpsimd
            def _(h):
                run("pool", h)

            @block.sync
            def _(h):
                run("sp", h)


class Ring:
    def __init__(self, bufs, toks=None):
        self.bufs = bufs
        self.toks = toks if toks is not None else [Tok() for _ in bufs]
        self.i = 0

    def next(self):
        b, t = self.bufs[self.i], self.toks[self.i]
        self.i = (self.i + 1) % len(self.bufs)
        return b, t


def build(L, TS, full_w_layers=None):
    NTG = 1 + TS // 512
    NT = NTG * 512
    NSQ = TS // 512
    nc = bass.Bass("TRN2", target_bir_lowering=False)
    st = contextlib.ExitStack()

    def din(name, shape, dt=F32):
        return nc.dram_tensor(name, list(shape), dt, kind="ExternalInput").ap()

    def dout(name, shape, dt=F32):
        return nc.dram_tensor(name, list(shape), dt, kind="ExternalOutput").ap()

    def dscr(name, shape, dt=F32):
        return nc.dram_tensor(name, list(shape), dt, kind="Internal").ap()

    xin = din("xin", [NT, D])
    cT = din("cT", [128, KC, 2])
    w_ada = din("w_ada", [L, D, 9 * D])
    b_adaT = din("b_adaT", [L, 128, 144])
    w_ffn_in = din("w_ffn_in", [L, 2, D, 2 * DFF])
    w_ffn_out = din("w_ffn_out", [L, 2, DFF, D])
    w_in = din("w_in", [L, D, INW])
    w_branch = din("w_branch", [L, 4, 512, D])
    w_out = din("w_out", [L, D, D])
    fnwT = din("fnwT", [128, KC])
    qkw = din("qkw", [128, L, 2])
    cache_k = din("cache_k", [L, 256, 2, 128])
    cache_v = din("cache_v", [L, 256, 2, 128])
    rope_cos = din("rope_cos", [128, TS])
    rope_sin = din("rope_sin", [128, TS])
    rmat = din("rmat", [128, 128])
    tri_in = din("tri", [2, 128, 128])
    gla_w2aug = din("gla_w2aug", [L, 2, 33, 512])
    state_gla = din("state_gla", [L, 2, 4, 128, 128])
    normw = din("normw", [128, L, 3])
    state_mc = din("state_mc", [L, 2, 4, 128, 128])
    state_gd = din("state_gd", [L, 2, 4, 128, 128])
    gd_cwT = din("gd_cwT", [L, 128, 12, 5])
    gd_gb = din("gd_gb", [128, L, 2, 8])
    strm_in = din("strm", [2, 128, 128])
    ml_n0T = din("ml_n0T", [L, 2, 128, 4])
    ml_m0b = din("ml_m0b", [128, L, 2, 4])
    ml_gb = din("ml_gb", [128, L, 2, 8])
    ident_in = din("ident", [128, 128])

    y_out = dout("y", [NT, D])
    nck = dout("nck", [2, L, 256, 2, 128])
    ncv = dout("ncv", [2, L, 256, 2, 128])
    nsg = dout("nsg", [2, L, 2, 4, 128, 128])
    nsc = dout("nsc", [2, L, 2, 4, 128, 128])
    nsn = dout("nsn", [2, L, 2, 4, 128])
    nsm = dout("nsm", [2, L, 2, 4])
    nsd = dout("nsd", [2, L, 2, 4, 128, 128])

    xT = dscr("xT", [D, NT])
    hidT = dscr("hidT", [DFF, NT], BF16)
    zT = dscr("zT", [INW, NT])
    ybrT = dscr("ybrT", [D, NT], BF16)
    oT = dscr("oT", [2, 1536, NT])
    gqkv = dscr("gqkv", [NT, 1536])
    w2bf = dscr("w2bf", [KC, 128, HC * 128], BF16)
    t_w2bf = {}

    def sb(name, shape, dt=F32):
        return st.enter_context(nc.sbuf_tensor(name, list(shape), dt))

    P = Prog(nc)

    hT = sb("hT", [128, KC, NT], BF16)
    hT_tok = [Tok("hT%d" % g) for g in range(NTG)]
    modT = sb("modT", [128, 144, 2])
    t_mod = Tok("mod")
    scT = sb("scT", [128, KC, 2], BF16)
    t_sc = Tok("sc")
    ident = sb("ident_sb", [128, 128])
    ones_bf = sb("ones_bf", [128, 128], BF16)
    rmat_bf = sb("rmat_bf", [128, 128], BF16)
    fnw = sb("fnw", [128, KC])
    qkw_sb = sb("qkw_sb", [128, L, 2])
    badd = sb("badd", [128, 144])
    t_const = Tok("const")
    zeros = sb("zeros", [128, 128])
    zeros_bf = sb("zeros_bf", [128, 512], BF16)

    _banks = [st.enter_context(nc.psum_tensor("ps%d" % i, [128, 512], F32)) for i in range(8)]
    psum = Ring(_banks[0:6])
    pacc = Ring(_banks[6:8])
    for _r in (psum, pacc):
        for _t in _r.toks:
            _t.psum = True
    arena = sb("arena_bf", [128, 33792], BF16)
    wring = Ring([arena[:, i * 4096:(i + 1) * 4096].rearrange("p (k c) -> p k c", k=KC) for i in range(4)])
    w2ring = Ring([arena[:, 22528 + i * 5632:22528 + (i + 1) * 5632].rearrange("p (k c) -> p k c", k=HC) for i in range(2)])
    hidin = arena[:, 0:22528].rearrange("p (k c) -> p k c", k=HC)
    t_hidin = Tok("hidin")
    xblk = Ring([sb("xb%d" % i, [128, 512]) for i in range(3)])
    tmpf = Ring([sb("tf%d" % i, [128, 512]) for i in range(3)])
    tmpb = Ring([sb("tb%d" % i, [128, 512], BF16) for i in range(3)])
    rstd = sb("rstd", [128, 512])
    t_rstd = Tok("rstd")

    t_xT = [Tok("xT%d" % g) for g in range(NTG)]
    t_hid = {}
    t_z = {}
    t_ybr = {}
    t_out = Tok("out")

    def tk(d, key):
        if key not in d:
            d[key] = Tok(str(key))
        return d[key]

    def ztoks(r0, r1, c0, c1):
        out = []
        for rb in range(r0 // 128, (r1 - 1) // 128 + 1):
            for g in range(c0 // 512, (c1 - 1) // 512 + 1):
                out.append(tk(t_z, (rb, g)))
        return out

    def mm(out, lhsT, rhs, start, stop, reads, writes):
        P.op("pe", lambda e: e.matmul(out, lhsT=lhsT, rhs=rhs, start=start, stop=stop), reads, writes)

    def tr(out, in_, reads, writes):
        kp = in_.shape[0]
        P.op("pe", lambda e: e.transpose(out, in_, ident[0:kp, 0:kp]), list(reads) + [t_const], writes)

    def act(out, in_, func, reads, writes, bias=None, scale=None):
        kw = {}
        if bias is not None:
            kw["bias"] = bias
        if scale is not None:
            kw["scale"] = scale
        P.op("act", lambda e: e.activation(out=out, in_=in_, func=func, **kw), reads, writes)

    def tt(out, in0, in1, op, reads, writes, eng="dve"):
        P.op(eng, lambda e: e.tensor_tensor(out=out, in0=in0, in1=in1, op=op), reads, writes)

    def ts(out, in0, s1, s2, op0, op1, reads, writes, eng="dve"):
        if op1 is None:
            P.op(eng, lambda e: e.tensor_scalar(out=out, in0=in0, scalar1=s1, scalar2=None, op0=op0), reads, writes)
        else:
            P.op(eng, lambda e: e.tensor_scalar(out=out, in0=in0, scalar1=s1, scalar2=s2, op0=op0, op1=op1), reads, writes)

    def stt(out, in0, scalar, in1, op0, op1, reads, writes):
        P.op("dve", lambda e: e.scalar_tensor_tensor(out=out, in0=in0, scalar=scalar, in1=in1, op0=op0, op1=op1), reads, writes)

    def cp(out, in_, reads, writes, eng="dve"):
        if eng == "act":
            P.op(eng, lambda e: e.activation(out=out, in_=in_, func=AF.Copy), reads, writes)
        else:
            P.op(eng, lambda e: e.tensor_copy(out=out, in_=in_), reads, writes)

    def recip(out, in_, reads, writes):
        P.op("dve", lambda e: e.reciprocal(out=out, in_=in_), reads, writes)

    def dma(out, in_, reads, writes, eng="sp"):
        P.dma(eng, lambda e: e.dma_start(out=out, in_=in_), reads, writes)

    def memset(ap, val, writes, eng="dve"):
        P.op(eng, lambda e: e.memset(ap, val), (), writes)

    dma(ident[:], ident_in, [], [t_const])
    dma(fnw[:], fnwT, [], [t_const])
    dma(qkw_sb[:], qkw, [], [t_const])
    P.dma("pool", lambda e: e.dma_start(out=rmat_bf[:], in_=rmat), [], [t_const])
    memset(ones_bf[:], 1.0, [t_const])
    memset(zeros[:], 0.0, [t_const])
    memset(zeros_bf[:], 0.0, [t_const])

    for g in range(NTG):
        for tb in range(4):
            xt, xtok = xblk.next()
            r0 = g * 512 + tb * 128
            for fq in range(4):
                xt, xtok = xblk.next()
                dma(xt[:], xin[r0:r0 + 128, fq * 512:(fq + 1) * 512], [], [xtok])
                ps, pt = psum.next()
                for j in range(4):
                    tr(ps[:, j * 128:(j + 1) * 128], xt[:, j * 128:(j + 1) * 128], [xtok], [pt])
                o, otok = tmpf.next()
                cp(o[:], ps[:], [pt], [otok], eng="act" if (tb + fq) % 2 else "dve")
                dma(xT[fq * 512:(fq + 1) * 512, r0:r0 + 128].rearrange("(j p) t -> p j t", p=128),
                    o[:].rearrange("p (j t) -> p j t", j=4), [otok], [t_xT[g]])

    def cond_of(g):
        return 0 if g == 0 else 1

    def rms_mod(sub):
        for g in range(NTG):
            c = cond_of(g)
            pss, pst = psum.next()
            for kc in range(KC):
                xb, xbt = xblk.next()
                dma(xb[:], xT[kc * 128:(kc + 1) * 128, g * 512:(g + 1) * 512], [t_xT[g]], [xbt])
                sq, sqt = tmpb.next()
                act(sq[:], xb[:], AF.Square, [xbt], [sqt])
                mm(pss[:], ones_bf[:], sq[:], kc == 0, kc == KC - 1, [sqt, t_const], [pst])
            act(rstd[:], pss[:], AF.Sqrt, [pst], [t_rstd], bias=eps_col[:], scale=1.0 / D)
            recip(rstd[:], rstd[:], [t_rstd], [t_rstd])
            for kc in range(KC):
                xb, xbt = xblk.next()
                dma(xb[:], xT[kc * 128:(kc + 1) * 128, g * 512:(g + 1) * 512], [t_xT[g]], [xbt])
                tf, tft = tmpf.next()
                tt(tf[:], xb[:], rstd[:], ALU.mult, [xbt, t_rstd], [tft])
                ts(hT[:, kc, g * 512:(g + 1) * 512], tf[:], modT[:, (3 * sub + 1) * 16 + kc, c:c + 1],
                   modT[:, (3 * sub) * 16 + kc, c:c + 1], ALU.mult, ALU.add, [tft, t_mod], [hT_tok[g]],
                   eng="pool" if kc % 2 else "dve")

    eps_col = sb("eps_col", [128, 1])
    memset(eps_col[:], EPS, [t_const])

    def resid_update(ps_ap, pt, m, g, gate_chunk0, ncols=512, c0=0):
        c = cond_of(g)
        xb, xbt = xblk.next()
        col0 = g * 512 + c0
        dma(xb[:, :ncols], xT[m * 128:(m + 1) * 128, col0:col0 + ncols], [t_xT[g]], [xbt])
        stt(xb[:, :ncols], ps_ap, modT[:, gate_chunk0 + m, c:c + 1], xb[:, :ncols], ALU.mult, ALU.add,
            [pt, xbt, t_mod], [xbt])
        dma(xT[m * 128:(m + 1) * 128, col0:col0 + ncols], xb[:, :ncols], [xbt], [t_xT[g]])

    def adaln(l):
        dma(badd[:], b_adaT[l], [], [t_mod])
        ctmp, ctt = tmpf.next()
        dma(ctmp[:, 0:KC * 2], cT.rearrange("p k c -> p (k c)"), [], [ctt])
        act(scT[:].rearrange("p k c -> p (k c)"), ctmp[:, 0:KC * 2], AF.Silu, [ctt], [t_sc])
        for blk in range(9 * D // 256):
            wt, wtok = wring.next()
            P.dma("pool", (lambda wt=wt, blk=blk: lambda e: e.dma_start(
                out=wt, in_=w_ada[l][:, blk * 256:(blk + 1) * 256].rearrange("(kc p) c -> p kc c", p=128)))(),
                [], [wtok])
            for mb in range(2):
                ps, pt = psum.next()
                for kc in range(KC):
                    mm(ps[:, 0:2], wt[:, kc, mb * 128:(mb + 1) * 128], scT[:, kc, :], kc == 0, kc == KC - 1,
                       [wtok, t_sc], [pt])
                j = blk * 2 + mb
                ts(modT[:, j, :], ps[:, 0:2], badd[:, j:j + 1], None, ALU.add, None, [pt, t_mod], [t_mod])
        for sub in range(3):
            ch = (3 * sub + 1) * 16
            ts(modT[:, ch:ch + 16, :], modT[:, ch:ch + 16, :], 1.0, None, ALU.add, None, [t_mod], [t_mod])
        for sub in (0, 2):
            ch = (3 * sub + 2) * 16
            ts(modT[:, ch:ch + 16, :], modT[:, ch:ch + 16, :], 0.5, None, ALU.mult, None, [t_mod], [t_mod])

    def ffn(l, f, sub):
        rms_mod(sub)
        W1 = w_ffn_in[l, f]
        for jb in range(DFF // 256):
            wg, wgt = wring.next()
            wu, wut = wring.next()
            P.dma("pool", (lambda wg=wg, jb=jb: lambda e: e.dma_start(
                out=wg, in_=W1[:, jb * 256:(jb + 1) * 256].rearrange("(kc p) c -> p kc c", p=128)))(), [], [wgt])
            P.dma("pool", (lambda wu=wu, jb=jb: lambda e: e.dma_start(
                out=wu, in_=W1[:, DFF + jb * 256:DFF + (jb + 1) * 256].rearrange("(kc p) c -> p kc c", p=128)))(), [], [wut])
            for g in range(NTG):
                for mb in range(2):
                    pg, pgt = psum.next()
                    pu, put = psum.next()
                    for kc in range(KC):
                        mm(pg[:], wg[:, kc, mb * 128:(mb + 1) * 128], hT[:, kc, g * 512:(g + 1) * 512],
                           kc == 0, kc == KC - 1, [wgt, hT_tok[g]], [pgt])
                    for kc in range(KC):
                        mm(pu[:], wu[:, kc, mb * 128:(mb + 1) * 128], hT[:, kc, g * 512:(g + 1) * 512],
                           kc == 0, kc == KC - 1, [wut, hT_tok[g]], [put])
                    sg, sgt = tmpf.next()
                    act(sg[:], pg[:], AF.Silu, [pgt], [sgt])
                    hb, hbt = tmpb.next()
                    tt(hb[:], sg[:], pu[:], ALU.mult, [sgt, put], [hbt])
                    j = jb * 2 + mb
                    dma(hidT[j * 128:(j + 1) * 128, g * 512:(g + 1) * 512], hb[:], [hbt], [tk(t_hid, (j, g))])
        W2 = w_ffn_out[l, f]
        gate0 = (3 * sub + 2) * 16
        P.barrier()
        for g in range(NTG):
            dma(hidin, hidT[:, g * 512:(g + 1) * 512].rearrange("(j p) t -> p j t", p=128),
                [tk(t_hid, (j, g)) for j in range(HC)], [t_hidin])
            for m in range(KC):
                wi_ = w2ring.i
                w2, w2t = w2ring.next()
                w2flat = arena[:, 22528 + wi_ * 5632:22528 + (wi_ + 1) * 5632]
                if g == 0:
                    P.dma("pool", (lambda w2=w2, m=m: lambda e: e.dma_start(
                        out=w2, in_=W2[:, m * 128:(m + 1) * 128].rearrange("(j p) c -> p j c", p=128)))(), [], [w2t])
                    dma(w2bf[m], w2flat, [w2t], [tk(t_w2bf, m)])
                else:
                    dma(w2flat, w2bf[m], [tk(t_w2bf, m)], [w2t], eng="act")
                ps, pt = psum.next()
                for j in range(HC):
                    mm(ps[:], w2[:, j, :], hidin[:, j, :], j == 0, j == HC - 1, [w2t, t_hidin], [pt])
                resid_update(ps[:], pt, m, g, gate0)
        P.barrier()

    def project(l):
        rms_mod(1)
        nblk = (INW + 255) // 256
        for blk in range(nblk):
            c0 = blk * 256
            wd = min(256, INW - c0)
            wt, wtok = wring.next()
            P.dma("pool", (lambda wt=wt, c0=c0, wd=wd: lambda e: e.dma_start(
                out=wt[:, :, 0:wd], in_=w_in[l][:, c0:c0 + wd].rearrange("(kc p) c -> p kc c", p=128)))(), [], [wtok])
            for g in range(NTG):
                for mb in range((wd + 127) // 128):
                    m = min(128, wd - mb * 128)
                    ps, pt = psum.next()
                    for kc in range(KC):
                        mm(ps[:m, :], wt[:, kc, mb * 128:mb * 128 + m], hT[:, kc, g * 512:(g + 1) * 512],
                           kc == 0, kc == KC - 1, [wtok, hT_tok[g]], [pt])
                    o, ot = tmpf.next()
                    cp(o[:m, :], ps[:m, :], [pt], [ot], eng="act" if (mb + g) % 2 else "dve")
                    r0 = c0 + mb * 128
                    dma(zT[r0:r0 + m, g * 512:(g + 1) * 512], o[:m, :], [ot], ztoks(r0, r0 + m, g * 512, (g + 1) * 512))

    NKB = 2 + TS // 128
    kT_all = arena[:, 0:256 + TS]
    v_all = arena[:, 2304:2304 + NKB * 128]
    qT_bf = arena[:, 4608:4608 + TS]
    t_kT, t_v, t_q = Tok("kT"), Tok("v"), Tok("q")

    def headnorm_cols(row0, col0, n, wcol, rope_c0, out_bf, out_tok, out_c0, keep_f32=None):
        xb, xbt = xblk.next()
        dma(xb[:, :n], zT[row0:row0 + 128, col0:col0 + n], ztoks(row0, row0 + 128, col0, col0 + n), [xbt])
        sq, sqt = tmpb.next()
        act(sq[:, :n], xb[:, :n], AF.Square, [xbt], [sqt])
        ps, pt = psum.next()
        mm(ps[:, :n], ones_bf[:], sq[:, :n], True, True, [sqt, t_const], [pt])
        rs, rst = tmpf.next()
        act(rs[:, :n], ps[:, :n], AF.Sqrt, [pt], [rst], bias=eps_col[:], scale=1.0 / HD)
        recip(rs[:, :n], rs[:, :n], [rst], [rst])
        tt(xb[:, :n], xb[:, :n], rs[:, :n], ALU.mult, [xbt, rst], [xbt])
        if rope_c0 is None:
            ts(out_bf[:, out_c0:out_c0 + n], xb[:, :n], wcol, None, ALU.mult, None, [xbt, t_const], [out_tok])
            if keep_f32 is not None:
                ts(keep_f32[0][:, :n], xb[:, :n], wcol, None, ALU.mult, None, [xbt, t_const], [keep_f32[1]])
            return
        ts(xb[:, :n], xb[:, :n], wcol, None, ALU.mult, None, [xbt, t_const], [xbt])
        nb, nbt = tmpb.next()
        cp(nb[:, :n], xb[:, :n], [xbt], [nbt], eng="act")
        ps2, pt2 = psum.next()
        mm(ps2[:, :n], rmat_bf[:], nb[:, :n], True, True, [nbt, t_const], [pt2])
        sn, snt = xblk.next()
        dma(sn[:, :n], rope_sin[:, rope_c0:rope_c0 + n], [], [snt])
        t1, t1t = tmpf.next()
        tt(t1[:, :n], ps2[:, :n], sn[:, :n], ALU.mult, [pt2, snt], [t1t])
        cs, cst = xblk.next()
        dma(cs[:, :n], rope_cos[:, rope_c0:rope_c0 + n], [], [cst])
        tt(xb[:, :n], xb[:, :n], cs[:, :n], ALU.mult, [xbt, cst], [xbt], eng="pool")
        tt(out_bf[:, out_c0:out_c0 + n], xb[:, :n], t1[:, :n], ALU.add, [xbt, t1t], [out_tok])

    def to_tokmajor_bf(row0, col0, nblk, out3, out_tok, blk0, dram_out=None):
        for i in range(0, nblk, 4):
            nb_ = min(4, nblk - i)
            xb, xbt = xblk.next()
            n = nb_ * 128
            dma(xb[:, :n], zT[row0:row0 + 128, col0 + i * 128:col0 + i * 128 + n],
                ztoks(row0, row0 + 128, col0 + i * 128, col0 + i * 128 + n), [xbt])
            ps, pt = psum.next()
            for j in range(nb_):
                tr(ps[:, j * 128:(j + 1) * 128], xb[:, j * 128:(j + 1) * 128], [xbt], [pt])
            if dram_out is not None:
                o, ot = tmpf.next()
                cp(o[:, :n], ps[:, :n], [pt], [ot], eng="act")
                cp(out3[:, (blk0 + i) * 128:(blk0 + i) * 128 + n], o[:, :n], [ot], [out_tok])
                for j in range(nb_):
                    dma(dram_out(i + j), o[:, j * 128:(j + 1) * 128], [ot], [t_out])
            else:
                cp(out3[:, (blk0 + i) * 128:(blk0 + i) * 128 + n], ps[:, :n], [pt], [out_tok])

    def attn_core(h, nq_cols, q_c0, nkb, ycol0):
        po, pot = pacc.next()
        pd, pdt = pacc.next()
        for kb in range(nkb):
            ps, pt = psum.next()
            mm(ps[:, :nq_cols], kT_all[:, kb * 128:(kb + 1) * 128], qT_bf[:, q_c0:q_c0 + nq_cols], True, True,
               [t_kT, t_q], [pt])
            pb, pbt = tmpb.next()
            act(pb[:, :nq_cols], ps[:, :nq_cols], AF.Exp, [pt], [pbt], scale=HD ** -0.5)
            mm(po[:, :nq_cols], v_all[:, kb * 128:(kb + 1) * 128], pb[:, :nq_cols], kb == 0, kb == nkb - 1, [t_v, pbt], [pot])
            mm(pd[:, :nq_cols], ones_bf[:], pb[:, :nq_cols], kb == 0, kb == nkb - 1, [t_const, pbt], [pdt])
        rd, rdt = tmpf.next()
        recip(rd[:, :nq_cols], pd[:, :nq_cols], [pdt], [rdt])
        yb, ybt = tmpb.next()
        tt(yb[:, :nq_cols], po[:, :nq_cols], rd[:, :nq_cols], ALU.mult, [pot, rdt], [ybt])
        r0 = 1536 + h * 128
        dma(ybrT[r0:r0 + 128, ycol0:ycol0 + nq_cols], yb[:, :nq_cols], [ybt],
            [tk(t_ybr, (r0 // 128, ycol0 // 512))])

    def attention(l):
        qcol = qkw_sb[:, l, 0:1]
        kcol = qkw_sb[:, l, 1:2]
        import os
        KAT = os.environ.get("KAT", "pkvcs")
        for s in (range(2) if "p" in KAT else []):
            col0 = s * 256
            for gk in range(2):
                kf, kft = tmpf.next()
                headnorm_cols(C_AT_K + gk * 128, col0, 256, kcol, None, kT_all, t_kT, 0, keep_f32=(kf, kft))
                if "k" in KAT:
                    ps, pt = psum.next()
                    for j in range(2):
                        tr(ps[:, j * 128:(j + 1) * 128], kf[:, j * 128:(j + 1) * 128], [kft], [pt])
                    o, ot = tmpf.next()
                    cp(o[:, :256], ps[:, :256], [pt], [ot], eng="act")
                    for j in range(2):
                        dma(nck[s, l, j * 128:(j + 1) * 128, gk, :], o[:, j * 128:(j + 1) * 128], [ot], [t_out])
                if "v" in KAT:
                    to_tokmajor_bf(C_AT_V + gk * 128, col0, 2, v_all, t_v, 0,
                                   dram_out=lambda i, s=s, gk=gk: ncv[s, l, i * 128:(i + 1) * 128, gk, :])
                for hh in range(2):
                    h = gk * 2 + hh
                    headnorm_cols(C_AT_Q + h * 128, col0, 256, qcol, None, qT_bf, t_q, 0)
                    if "c" in KAT:
                        attn_core(h, 256, 0, 2, col0)
        for gk in (range(2) if "s" in KAT else []):
            ck, ckt = tmpf.next()
            dma(ck[:, :256].rearrange("p (j e) -> p j e", j=2),
                cache_k[l, :, gk, :].rearrange("(j p) e -> p j e", p=128), [], [ckt])
            ps, pt = psum.next()
            for j in range(2):
                tr(ps[:, j * 128:(j + 1) * 128], ck[:, j * 128:(j + 1) * 128], [ckt], [pt])
            cp(kT_all[:, 0:256], ps[:, 0:256], [pt], [t_kT])
            cv, cvt = tmpf.next()
            dma(cv[:, :256].rearrange("p (j e) -> p j e", j=2),
                cache_v[l, :, gk, :].rearrange("(j p) e -> p j e", p=128), [], [cvt])
            cp(v_all[:, 0:256], cv[:, :256], [cvt], [t_v])
            for qg in range(NSQ):
                headnorm_cols(C_AT_K + gk * 128, 512 + qg * 512, 512, kcol, qg * 512, kT_all, t_kT, 256 + qg * 512)
            to_tokmajor_bf(C_AT_V + gk * 128, 512, TS // 128, v_all, t_v, 2)
            for hh in range(2):
                h = gk * 2 + hh
                for qg in range(NSQ):
                    headnorm_cols(C_AT_Q + h * 128, 512 + qg * 512, 512, qcol, qg * 512, qT_bf, t_q, qg * 512)
                for qg in range(NSQ):
                    attn_core(h, 512, qg * 512, 2 + TS // 128, 512 + qg * 512)


    rf = Ring([arena[:, i * 1024:(i + 1) * 1024].bitcast(F32) for i in range(22)])
    rb = Ring([arena[:, 22528 + i * 512:22528 + (i + 1) * 512] for i in range(16)])
    L4 = [sb("L4_%d" % d, [128, 512]) for d in range(2)]
    for d in range(2):
        for j in range(4):
            dma(L4[d][:, j * 128:(j + 1) * 128], tri_in[d], [], [t_const])
    normw_sb = sb("normw_sb", [128, L, 3])
    dma(normw_sb[:], normw, [], [t_const])
    ones_f = sb("ones_f", [128, 128])
    memset(ones_f[:], 1.0, [t_const])
    w2a = sb("w2a", [33, 2, 512])
    t_w2a = Tok("w2a")
    t_o = {}


    class _Chain:
        pass

    CH = []
    for ci_ in range(2):
        R = _Chain()
        R.psum = Ring(_banks[ci_ * 4:(ci_ + 1) * 4], (psum.toks + pacc.toks)[ci_ * 4:(ci_ + 1) * 4])
        R.lra = Ring([sb("lra%d_%d" % (ci_, i), [33, 128]) for i in range(2)])
        for b_ in R.lra.bufs:
            memset(b_[:], 0.0, [t_const])
            memset(b_[32:33, :], 1.0, [t_const])
        R.Sst = sb("Sst%d" % ci_, [128, 4, 132])
        R.Sbf = sb("Sbf%d" % ci_, [128, 4, 132], BF16)
        R.t_S, R.t_Sbf = Tok("S%d" % ci_), Tok("Sbf%d" % ci_)
        R.gsm = Ring([sb("gsm%d_%d" % (ci_, i), [128, 128]) for i in range(4)])
        R.vaug = Ring([sb("vaug%d_%d" % (ci_, i), [128, 4, 132], BF16) for i in range(2)])
        for b_ in R.vaug.bufs:
            memset(b_[:], 1.0, [t_const])
        R.nbf = sb("nbf%d" % ci_, [128, 4, 128], BF16)
        R.t_nbf = Tok("nbf%d" % ci_)
        R.mrow = sb("mrow%d" % ci_, [4, 2])
        R.t_mrow = Tok("mrow%d" % ci_)
        CH.append(R)

    def run_chains(gens):
        live = list(gens)
        while live:
            nxt = []
            for g_ in live:
                try:
                    next(g_)
                    nxt.append(g_)
                except StopIteration:
                    pass
            live = nxt

    def seqs():
        return [("p", 0, 0, 256), ("p", 1, 256, 256), ("s", 0, 512, TS)]

    def load4(row0, c0, n=128):
        t, tt_ = rf.next()
        dma(t[:, :4 * n].rearrange("p (h t) -> p h t", h=4),
            zT[row0:row0 + 512, c0:c0 + n].rearrange("(h p) t -> p h t", p=128),
            ztoks(row0, row0 + 512, c0, c0 + n), [tt_])
        return t, tt_

    def tr4(src, srct, dst_bf, dstt, ring=None):
        ps, pt = (ring or psum).next()
        for h in range(4):
            tr(ps[:, h * 128:(h + 1) * 128], src[:, h * 128:(h + 1) * 128], [srct], [pt])
        cp(dst_bf[:, :512], ps[:], [pt], [dstt], eng="act")

    def gla_chain(l, kind, sq, col0, T, d, R):
        if True:
            ncn = T // 128
            if True:
                psum, lra, Sst, Sbf, t_S, t_Sbf = R.psum, R.lra, R.Sst, R.Sbf, R.t_S, R.t_Sbf
                end = 127 if d == 0 else 0
                if kind == "p":
                    memset(Sst[:], 0.0, [t_S])
                    memset(Sbf[:], 0.0, [t_Sbf])
                else:
                    dma(Sst[:, :, 0:128], state_gla[l, d].rearrange("h k e -> k h e"), [], [t_S])
                    yield
                    cp(Sbf[:, :, 0:128], Sst[:, :, 0:128], [t_S], [t_Sbf], eng="act")
                    yield
                for ci in (range(ncn) if d == 0 else range(ncn - 1, -1, -1)):
                    c0 = col0 + ci * 128
                    lr_, lrt = lra.next()
                    dma(lr_[0:16, :], zT[C_GLA_LR + d * 16:C_GLA_LR + d * 16 + 16, c0:c0 + 128],
                        ztoks(C_GLA_LR, C_GLA_LR + 32, c0, c0 + 128), [lrt])
                    pp, ppt = psum.next()
                    mm(pp[:], lr_[0:33, :], w2a[0:33, d, :], True, True, [lrt, t_w2a, t_const], [ppt])
                    e1, e1t = rf.next()
                    act(e1[:], pp[:], AF.Exp, [ppt], [e1t], scale=-1.0)
                    yield
                    sp_, spt = rf.next()
                    act(sp_[:], e1[:], AF.Ln, [e1t], [spt], bias=one_col[:], scale=1.0)
                    yield
                    pB, pBt = psum.next()
                    for h in range(4):
                        mm(pB[:, h * 128:(h + 1) * 128], sp_[:, h * 128:(h + 1) * 128], L4[d][:, 0:128], True, True,
                           [spt, t_const], [pBt])
                    epos, epost = rf.next()
                    act(epos[:], pB[:], AF.Exp, [pBt], [epost], scale=-1.0 / 16)
                    yield
                    eneg, enegt = rf.next()
                    act(eneg[:], pB[:], AF.Exp, [pBt], [enegt], scale=1.0 / 16)
                    yield
                    q4, q4t = load4(C_GLA_Q, c0)
                    k4, k4t = load4(C_GLA_K, c0)
                    v4, v4t = load4(C_GLA_V, c0)
                    qs, qst = rb.next()
                    stt(qs[:], q4[:], HD ** -0.5, epos[:], ALU.mult, ALU.mult, [q4t, epost], [qst])
                    yield
                    tt(k4[:], k4[:], eneg[:], ALU.mult, [k4t, enegt], [k4t], eng="pool")
                    yield
                    ks, kst = rb.next()
                    cp(ks[:], k4[:], [k4t], [kst], eng="act")
                    yield
                    kh, kht = rf.next()
                    for h in range(4):
                        ts(kh[:, h * 128:(h + 1) * 128], k4[:, h * 128:(h + 1) * 128],
                           epos[:, h * 128 + end:h * 128 + end + 1], None, ALU.mult, None, [k4t, epost], [kht], eng="pool")
                    khT, khTt = rb.next()
                    tr4(kh, kht, khT, khTt, psum)
                    yield
                    vT, vTt = rb.next()
                    tr4(v4, v4t, vT, vTt, psum)
                    yield
                    pA, pAt = psum.next()
                    for h in range(4):
                        mm(pA[:, h * 128:(h + 1) * 128], ks[:, h * 128:(h + 1) * 128], qs[:, h * 128:(h + 1) * 128],
                           True, True, [kst, qst], [pAt])
                    am, amt = rb.next()
                    tt(am[:], pA[:], L4[d][:], ALU.mult, [pAt, t_const], [amt])
                    yield
                    pO, pOt = psum.next()
                    for h in range(4):
                        mm(pO[:, h * 128:(h + 1) * 128], vT[:, h * 128:(h + 1) * 128], am[:, h * 128:(h + 1) * 128],
                           True, False, [vTt, amt], [pOt])
                        mm(pO[:, h * 128:(h + 1) * 128], Sbf[:, h, 0:128], qs[:, h * 128:(h + 1) * 128],
                           False, True, [t_Sbf, qst], [pOt])
                    ob, obt = rf.next()
                    cp(ob[:], pO[:], [pOt], [obt], eng="act")
                    yield
                    dma(oT[d, 0:512, c0:c0 + 128].rearrange("(h p) t -> p h t", p=128),
                        ob[:].rearrange("p (h t) -> p h t", h=4), [obt], [tk(t_o, (d, 0, c0 // 512))], eng="act")
                    pU, pUt = psum.next()
                    for h in range(4):
                        mm(pU[:, h * 128:(h + 1) * 128], khT[:, h * 128:(h + 1) * 128], vT[:, h * 128:(h + 1) * 128],
                           True, True, [khTt, vTt], [pUt])
                    for h in range(4):
                        stt(Sst[:, h, 0:128], Sst[:, h, 0:128], epos[:, h * 128 + end:h * 128 + end + 1],
                            pU[:, h * 128:(h + 1) * 128], ALU.mult, ALU.add, [t_S, epost, pUt], [t_S])
                    cp(Sbf[:, :, 0:128], Sst[:, :, 0:128], [t_S], [t_Sbf], eng="act")
                    yield
                if kind == "p":
                    dma(nsg[sq, l, d].rearrange("h k e -> k h e"), Sst[:, :, 0:128], [t_S], [t_out])
                    yield


    def gla(l):
        dma(w2a[:], gla_w2aug[l].rearrange("d r c -> r d c"), [], [t_w2a])
        for (kind, sq, col0, T) in seqs():
            run_chains([gla_chain(l, kind, sq, col0, T, d, CH[d]) for d in range(2)])

    def post(l, mix, grow, gfunc):
        for h in range(4):
            for g in range(NTG):
                r0 = mix * 512 + h * 128
                a, at = rf.next()
                b2_, bt = rf.next()
                dma(a[:], oT[0, r0:r0 + 128, g * 512:(g + 1) * 512], [tk(t_o, (0, mix, g))], [at])
                dma(b2_[:], oT[1, r0:r0 + 128, g * 512:(g + 1) * 512], [tk(t_o, (1, mix, g))], [bt])
                tt(a[:], a[:], b2_[:], ALU.add, [at, bt], [at])
                sq_, sqt = rb.next()
                act(sq_[:], a[:], AF.Square, [at], [sqt])
                ps, pt = psum.next()
                mm(ps[:], ones_bf[:], sq_[:], True, True, [sqt, t_const], [pt])
                rs, rst = rf.next()
                act(rs[:], ps[:], AF.Sqrt, [pt], [rst], bias=eps_col[:], scale=1.0 / HD)
                recip(rs[:], rs[:], [rst], [rst])
                stt(a[:], a[:], normw_sb[:, l, mix:mix + 1], rs[:], ALU.mult, ALU.mult, [at, rst, t_const], [at])
                zg, zgt = rf.next()
                gr = grow + h * 128
                dma(zg[:], zT[gr:gr + 128, g * 512:(g + 1) * 512], ztoks(gr, gr + 128, g * 512, (g + 1) * 512), [zgt])
                act(zg[:], zg[:], gfunc, [zgt], [zgt])
                yb, ybt = rb.next()
                tt(yb[:], a[:], zg[:], ALU.mult, [at, zgt], [ybt], eng="pool")
                dma(ybrT[r0:r0 + 128, g * 512:(g + 1) * 512], yb[:], [ybt], [tk(t_ybr, (r0 // 128, g))])


    gsm = Ring([sb("gsm%d" % i, [128, 128]) for i in range(4)])
    mlc = sb("mlc", [128, L, 2, 12])
    dma(mlc[:, :, :, 0:8], ml_gb, [], [t_const])
    dma(mlc[:, :, :, 8:12], ml_m0b, [], [t_const])

    def mlstm_chain(l, kind, sq, col0, T, d, R):
        if True:
            ncn = T // 128
            if True:
                psum, pacc, Sst, Sbf, t_S, t_Sbf = R.psum, R.psum, R.Sst, R.Sbf, R.t_S, R.t_Sbf
                gsm, vaug, nbf, t_nbf, mrow, t_mrow = R.gsm, R.vaug, R.nbf, R.t_nbf, R.mrow, R.t_mrow
                if kind == "p":
                    memset(Sst[:], 0.0, [t_S])
                    memset(mrow[:], 0.0, [t_mrow])
                else:
                    dma(Sst[:, :, 0:128], state_mc[l, d].rearrange("h k e -> k h e"), [], [t_S])
                    yield
                    n0, n0t = gsm.next()
                    dma(n0[:, 0:4], ml_n0T[l, d], [], [n0t])
                    yield
                    act(n0[:, 4:8], mlc[:, l, d, 8:12], AF.Exp, [t_const, n0t], [n0t])
                    yield
                    cp(Sst[:, :, 128:129], n0[:, 0:4].rearrange("p (h o) -> p h o", o=1), [n0t, t_S], [t_S])
                    yield
                    for h in range(4):
                        ts(Sst[:, h, 0:129], Sst[:, h, 0:129], n0[:, 4 + h:5 + h], None, ALU.mult, None, [t_S, n0t], [t_S])
                cp(Sbf[:, :, 0:128], Sst[:, :, 0:128], [t_S], [t_Sbf], eng="act")
                for h in range(4):
                    ts(nbf[:, h, :], ones_f[:], Sst[:, h, 128:129], None, ALU.mult, None, [t_S, t_const], [t_nbf], eng="pool")
                    yield
                for ci in (range(ncn) if d == 0 else range(ncn - 1, -1, -1)):
                    c0 = col0 + ci * 128
                    gT, gTt = gsm.next()
                    r0 = C_ML_IF + d * 8
                    dma(gT[0:8, 0:128], zT[r0:r0 + 8, c0:c0 + 128], ztoks(r0, r0 + 8, c0, c0 + 128), [gTt])
                    yield
                    pg, pgt = psum.next()
                    tr(pg[:, 0:8], gT[0:8, 0:128], [gTt], [pgt])
                    G, Gt = gsm.next()
                    tt(G[:, 0:8], pg[:, 0:8], mlc[:, l, d, 0:8], ALU.add, [pgt, t_const], [Gt])
                    yield
                    act(G[:, 8:12], G[:, 4:8], AF.Exp, [Gt], [Gt], scale=-1.0)
                    yield
                    act(G[:, 8:12], G[:, 8:12], AF.Ln, [Gt], [Gt], bias=one_col[:], scale=1.0)
                    yield
                    pF, pFt = psum.next()
                    mm(pF[:, 0:4], L4[d][:, 0:128], G[:, 8:12], True, True, [Gt, t_const], [pFt])
                    mm(pF[:, 4:8], ones_f[:], G[:, 8:12], True, True, [Gt, t_const], [pFt])
                    cp(G[:, 32:40], pF[:, 0:8], [pFt, Gt], [Gt])
                    yield
                    tt(G[:, 12:16], G[:, 0:4], G[:, 32:36], ALU.add, [Gt], [Gt])
                    yield
                    act(G[:, 16:20], G[:, 12:16], AF.Exp, [Gt], [Gt])
                    yield
                    act(G[:, 20:24], G[:, 32:36], AF.Exp, [Gt], [Gt])
                    yield
                    act(G[:, 24:28], G[:, 36:40], AF.Exp, [Gt], [Gt], scale=-1.0)
                    yield
                    tt(G[:, 28:32], G[:, 16:20], G[:, 24:28], ALU.mult, [Gt], [Gt])
                    yield
                    if kind == "p":
                        pm, pmt = psum.next()
                        tr(pm[0:4, 0:128], G[:, 12:16], [Gt], [pmt])
                        mx, mxt = gsm.next()
                        P.op("dve", (lambda mx=mx, pm=pm: lambda e: e.reduce_max(out=mx[0:4, 0:1], in_=pm[0:4, 0:128],
                                                                             axis=mybir.AxisListType.X))(), [pmt], [mxt])
                        pm2, pm2t = psum.next()
                        tr(pm2[0:4, 0:128], G[:, 36:40], [Gt], [pm2t])
                        tt(mrow[0:4, 0:1], mrow[0:4, 0:1], mx[0:4, 0:1], ALU.max, [t_mrow, mxt], [t_mrow])
                        tt(mrow[0:4, 0:1], mrow[0:4, 0:1], pm2[0:4, 0:1], ALU.subtract, [t_mrow, pm2t], [t_mrow])
                    q4, q4t = load4(C_ML_Q, c0)
                    k4, k4t = load4(C_ML_K, c0)
                    v4, v4t = load4(C_ML_V, c0)
                    ts(k4[:], k4[:], HD ** -0.5, None, ALU.mult, None, [k4t], [k4t], eng="pool")
                    yield
                    qb, qbt = rb.next()
                    cp(qb[:], q4[:], [q4t], [qbt], eng="act")
                    yield
                    kb, kbt = rb.next()
                    cp(kb[:], k4[:], [k4t], [kbt], eng="act")
                    yield
                    pA, pAt = psum.next()
                    for h in range(4):
                        mm(pA[:, h * 128:(h + 1) * 128], kb[:, h * 128:(h + 1) * 128], qb[:, h * 128:(h + 1) * 128],
                           True, True, [kbt, qbt], [pAt])
                    am, amt = rb.next()
                    for h in range(4):
                        stt(am[:, h * 128:(h + 1) * 128], pA[:, h * 128:(h + 1) * 128], G[:, 16 + h:17 + h], L4[d][:, 0:128],
                            ALU.mult, ALU.mult, [pAt, Gt, t_const], [amt])
                    pK, pKt = psum.next()
                    for h in range(4):
                        tr(pK[:, h * 128:(h + 1) * 128], k4[:, h * 128:(h + 1) * 128], [k4t], [pKt])
                    kc, kct = rb.next()
                    for h in range(4):
                        ts(kc[:, h * 128:(h + 1) * 128], pK[:, h * 128:(h + 1) * 128], G[:, 28 + h:29 + h], None, ALU.mult, None,
                           [pKt, Gt], [kct])
                    pV, pVt = psum.next()
                    for h in range(4):
                        tr(pV[:, h * 128:(h + 1) * 128], v4[:, h * 128:(h + 1) * 128], [v4t], [pVt])
                    va, vat = vaug.next()
                    cp(va[:, :, 0:128], pV[:].rearrange("p (h e) -> p h e", h=4), [pVt], [vat], eng="act")
                    yield
                    pO, pOt = psum.next()
                    pD, pDt = pacc.next()
                    pE, pEt = pacc.next()
                    dg, dgt = rf.next()
                    for h in range(4):
                        mm(pO[:, h * 128:(h + 1) * 128], va[:, h, 0:128], am[:, h * 128:(h + 1) * 128], True, False, [vat, amt], [pOt])
                        mm(pO[:, h * 128:(h + 1) * 128], Sbf[:, h, 0:128], qb[:, h * 128:(h + 1) * 128], False, True, [t_Sbf, qbt], [pOt])
                        mm(pD[:, h * 128:(h + 1) * 128], ones_bf[:], am[:, h * 128:(h + 1) * 128], True, False, [t_const, amt], [pDt])
                        mm(pD[:, h * 128:(h + 1) * 128], nbf[:, h, :], qb[:, h * 128:(h + 1) * 128], False, True, [t_nbf, qbt], [pDt])
                        ts(dg[:, h * 128:(h + 1) * 128], ident[:], G[:, 20 + h:21 + h], None, ALU.mult, None, [Gt, t_const], [dgt])
                        mm(pE[:, h * 128:(h + 1) * 128], ones_f[:], dg[:, h * 128:(h + 1) * 128], True, True, [t_const, dgt], [pEt])
                    bcs, bcst = rf.next()
                    cp(bcs[:], pE[:], [pEt], [bcst], eng="act")
                    yield
                    dab, dabt = rf.next()
                    act(dab[:], pD[:], AF.Abs, [pDt], [dabt])
                    yield
                    tt(dab[:], dab[:], bcs[:], ALU.max, [dabt, bcst], [dabt])
                    yield
                    recip(dab[:], dab[:], [dabt], [dabt])
                    yield
                    ob, obt = rf.next()
                    tt(ob[:], pO[:], dab[:], ALU.mult, [pOt, dabt], [obt])
                    yield
                    dma(oT[d, 512:1024, c0:c0 + 128].rearrange("(h p) t -> p h t", p=128),
                        ob[:].rearrange("p (h t) -> p h t", h=4), [obt], [tk(t_o, (d, 1, c0 // 512))])
                    pU1, pU1t = psum.next()
                    pU2, pU2t = psum.next()
                    for h in range(4):
                        pu, put = (pU1, pU1t) if h < 2 else (pU2, pU2t)
                        o_ = (h % 2) * 132
                        mm(pu[:, o_:o_ + 129], kc[:, h * 128:(h + 1) * 128], va[:, h, 0:129], True, True, [kct, vat], [put])
                    for h in range(4):
                        pu, put = (pU1, pU1t) if h < 2 else (pU2, pU2t)
                        o_ = (h % 2) * 132
                        stt(Sst[:, h, 0:129], Sst[:, h, 0:129], G[:, 24 + h:25 + h], pu[:, o_:o_ + 129], ALU.mult, ALU.add,
                            [t_S, Gt, put], [t_S])
                    cp(Sbf[:, :, 0:128], Sst[:, :, 0:128], [t_S], [t_Sbf], eng="act")
                    yield
                    for h in range(4):
                        ts(nbf[:, h, :], ones_f[:], Sst[:, h, 128:129], None, ALU.mult, None, [t_S, t_const], [t_nbf], eng="pool")
                if kind == "p":
                    dgm, dgmt = gsm.next()
                    ts(dgm[0:4, 0:4], ident[0:4, 0:4], mrow[0:4, 0:1], None, ALU.mult, None, [t_mrow, t_const], [dgmt])
                    yield
                    pmb, pmbt = psum.next()
                    mm(pmb[:, 0:4], ones_f[0:4, :], dgm[0:4, 0:4], True, True, [dgmt, t_const], [pmbt])
                    act(dgm[:, 8:12], pmb[:, 0:4], AF.Exp, [pmbt, dgmt], [dgmt], scale=-1.0)
                    yield
                    so, sot = rf.next()
                    for h in range(4):
                        ts(so[:, h * 128:(h + 1) * 128], Sst[:, h, 0:128], dgm[:, 8 + h:9 + h], None, ALU.mult, None, [t_S, dgmt], [sot])
                    tt(dgm[:, 12:16], Sst[:, :, 128:129].rearrange("p h o -> p (h o)"), dgm[:, 8:12], ALU.mult, [t_S, dgmt], [dgmt])
                    yield
                    dma(nsc[sq, l, d].rearrange("h k e -> k h e"), so[:].rearrange("p (h e) -> p h e", h=4), [sot], [t_out])
                    yield
                    pn, pnt = psum.next()
                    tr(pn[0:4, 0:128], dgm[:, 12:16], [dgmt], [pnt])
                    nso, nsot = gsm.next()
                    cp(nso[0:4, 0:128], pn[0:4, 0:128], [pnt], [nsot])
                    yield
                    dma(nsn[sq, l, d], nso[0:4, 0:128], [nsot], [t_out])
                    yield
                    dma(nsm[sq, l, d:d + 1, :].rearrange("o h -> h o"), mrow[0:4, 0:1], [t_mrow], [t_out])
                    yield


    def mlstm(l):
        for (kind, sq, col0, T) in seqs():
            run_chains([mlstm_chain(l, kind, sq, col0, T, d, CH[d]) for d in range(2)])

    GT = [arena[:, i * 1024:(i + 1) * 1024].bitcast(F32) for i in range(33)]
    GTt = [Tok("gt%d" % i) for i in range(33)]
    cx = arena[:, 22528:22528 + 4112].bitcast(F32)
    cacc = arena[:, 26640:26640 + 4096].bitcast(F32)
    t_cx, t_cacc = Tok("cx"), Tok("cacc")
    S4 = [sb("S4_%d" % d, [128, 512]) for d in range(2)]
    I4 = sb("I4", [128, 512])
    for j in range(4):
        for d in range(2):
            dma(S4[d][:, j * 128:(j + 1) * 128], strm_in[d], [], [t_const])
        dma(I4[:, j * 128:(j + 1) * 128], ident_in, [], [t_const])
    cw = sb("cw", [128, 12, 5])
    t_cw = Tok("cw")
    gdc = sb("gdc", [128, L, 2, 8])
    dma(gdc[:], gd_gb, [], [t_const])
    t_gq = {}

    def gdn_conv(l):
        dma(cw[:], gd_cwT[l], [], [t_cw])
        for (kind, sq, col0, T) in seqs():
            for blk in range(12):
                part = blk // 4
                r0 = C_GD_QKV + blk * 128
                memset(cx[:, 0:2], 0.0, [t_cx])
                memset(cx[:, T + 2:T + 4], 0.0, [t_cx])
                dma(cx[:, 2:T + 2], zT[r0:r0 + 128, col0:col0 + T], ztoks(r0, r0 + 128, col0, col0 + T), [t_cx])
                ts(cacc[:, 0:T], cx[:, 0:T], cw[:, blk, 0:1], None, ALU.mult, None, [t_cx, t_cw], [t_cacc])
                for j in range(1, 5):
                    stt(cacc[:, 0:T], cx[:, j:j + T], cw[:, blk, j:j + 1], cacc[:, 0:T], ALU.mult, ALU.add,
                        [t_cx, t_cw, t_cacc], [t_cacc])
                act(cacc[:, 0:T], cacc[:, 0:T], AF.Silu, [t_cacc], [t_cacc])
                for pc in range(0, T, 512):
                    n = min(512, T - pc)
                    if part < 2:
                        sq_, sqt = tmpb.next()
                        act(sq_[:, :n], cacc[:, pc:pc + n], AF.Square, [t_cacc], [sqt])
                        ps, pt = psum.next()
                        mm(ps[:, :n], ones_bf[:], sq_[:, :n], True, True, [sqt, t_const], [pt])
                        rs, rst = tmpf.next()
                        act(rs[:, :n], ps[:, :n], AF.Sqrt, [pt], [rst], bias=eps_col[:], scale=1.0)
                        recip(rs[:, :n], rs[:, :n], [rst], [rst])
                        if part == 0:
                            stt(cacc[:, pc:pc + n], cacc[:, pc:pc + n], HD ** -0.5, rs[:, :n], ALU.mult, ALU.mult,
                                [t_cacc, rst], [t_cacc])
                        else:
                            tt(cacc[:, pc:pc + n], cacc[:, pc:pc + n], rs[:, :n], ALU.mult, [t_cacc, rst], [t_cacc])
                    ps2, pt2 = psum.next()
                    nb_ = n // 128
                    for j in range(nb_):
                        tr(ps2[:, j * 128:(j + 1) * 128], cacc[:, pc + j * 128:pc + (j + 1) * 128], [t_cacc], [pt2])
                    o, ot = tmpf.next()
                    cp(o[:, :n], ps2[:, :n], [pt2], [ot], eng="act")
                    tok0 = col0 + pc
                    dma(gqkv[tok0:tok0 + n, blk * 128:(blk + 1) * 128].rearrange("(j t) c -> t j c", t=128),
                        o[:, :n].rearrange("t (j c) -> t j c", j=nb_), [ot], [tk(t_gq, (tok0 // 512, blk))])

    def gdn(l):
        (qtok, ktok, vtok, kp, kb, qp, rv, kh, kT, kbT, qT, qpT, X, XT, X2, XT2, RT, AT, solv, solk, PT, McT, Sg, obg, Dm, DTs, DTi, tmpD) = range(28)

        def g4(i, h):
            return GT[i][:, h * 128:(h + 1) * 128]

        for (kind, sq, col0, T) in seqs():
            ncn = T // 128
            for d in range(2):
                if kind == "p":
                    memset(GT[Sg][:], 0.0, [GTt[Sg]])
                else:
                    dma(GT[Sg][:].rearrange("k (h e) -> k h e", h=4), state_gd[l, d].rearrange("h k e -> k h e"), [], [GTt[Sg]])
                for ci in (range(ncn) if d == 0 else range(ncn - 1, -1, -1)):
                    c0 = col0 + ci * 128
                    gT_, gTt_ = gsm.next()
                    r0 = C_GD_AB + d * 8
                    dma(gT_[0:8, 0:128], zT[r0:r0 + 8, c0:c0 + 128], ztoks(r0, r0 + 8, c0, c0 + 128), [gTt_])
                    pg, pgt = psum.next()
                    tr(pg[:, 0:8], gT_[0:8, 0:128], [gTt_], [pgt])
                    G, Gt = gsm.next()
                    cp(G[:, 0:8], pg[:, 0:8], [pgt], [Gt])
                    tt(G[:, 8:12], G[:, 0:4], gdc[:, l, d, 0:4], ALU.add, [Gt, t_const], [Gt])
                    act(G[:, 8:12], G[:, 8:12], AF.Exp, [Gt], [Gt])
                    act(G[:, 8:12], G[:, 8:12], AF.Ln, [Gt], [Gt], bias=one_col[:], scale=1.0)
                    act(G[:, 12:16], gdc[:, l, d, 4:8], AF.Exp, [Gt, t_const], [Gt])
                    tt(G[:, 8:12], G[:, 8:12], G[:, 12:16], ALU.mult, [Gt], [Gt])
                    act(G[:, 16:20], G[:, 4:8], AF.Exp, [Gt], [Gt], scale=-1.0)
                    ts(G[:, 16:20], G[:, 16:20], 1.0, None, ALU.add, None, [Gt], [Gt])
                    recip(G[:, 16:20], G[:, 16:20], [Gt], [Gt])
                    pF, pFt = psum.next()
                    mm(pF[:, 0:4], L4[d][:, 0:128], G[:, 8:12], True, True, [Gt, t_const], [pFt])
                    mm(pF[:, 4:8], ones_f[:], G[:, 8:12], True, True, [Gt, t_const], [pFt])
                    cp(G[:, 20:28], pF[:, 0:8], [pFt, Gt], [Gt])
                    act(G[:, 28:32], G[:, 20:24], AF.Exp, [Gt], [Gt], scale=-1.0)
                    act(G[:, 36:40], G[:, 24:28], AF.Exp, [Gt], [Gt], scale=-1.0)
                    tt(G[:, 40:44], G[:, 16:20], G[:, 28:32], ALU.mult, [Gt], [Gt])
                    tt(G[:, 44:48], G[:, 24:28], G[:, 20:24], ALU.subtract, [Gt], [Gt])
                    act(G[:, 44:48], G[:, 44:48], AF.Exp, [Gt], [Gt], scale=-1.0)
                    pGb, pGbt = psum.next()
                    for h in range(4):
                        ts(g4(tmpD, h), ident[:], G[:, 20 + h:21 + h], None, ALU.mult, None, [Gt, t_const], [GTt[tmpD]])
                        mm(pGb[:, h * 128:(h + 1) * 128], ones_f[:], g4(tmpD, h), True, True, [GTt[tmpD], t_const], [pGbt])
                    for h in range(4):
                        ts(g4(Dm, h), pGb[:, h * 128:(h + 1) * 128], G[:, 20 + h:21 + h], 0.0, ALU.subtract, ALU.min, [pGbt, Gt], [GTt[Dm]])
                        ts(g4(DTs, h), pGb[:, h * 128:(h + 1) * 128], G[:, 20 + h:21 + h], 0.0, ALU.subtract, ALU.max, [pGbt, Gt], [GTt[DTs]])
                    act(GT[Dm][:], GT[Dm][:], AF.Exp, [GTt[Dm]], [GTt[Dm]])
                    act(GT[DTs][:], GT[DTs][:], AF.Exp, [GTt[DTs]], [GTt[DTs]], scale=-1.0)
                    tt(GT[Dm][:], GT[Dm][:], S4[1 - d][:], ALU.mult, [GTt[Dm], t_const], [GTt[Dm]], eng="pool")
                    tt(GT[DTi][:], GT[DTs][:], L4[d][:], ALU.mult, [GTt[DTs], t_const], [GTt[DTi]], eng="pool")
                    tt(GT[DTs][:], GT[DTs][:], S4[d][:], ALU.mult, [GTt[DTs], t_const], [GTt[DTs]], eng="pool")
                    for i, part in ((qtok, 0), (ktok, 1), (vtok, 2)):
                        dma(GT[i][:], gqkv[c0:c0 + 128, part * 512:(part + 1) * 512],
                            [tk(t_gq, (c0 // 512, part * 4 + h)) for h in range(4)], [GTt[i]])
                    for h in range(4):
                        ts(g4(kp, h), g4(ktok, h), G[:, 40 + h:41 + h], None, ALU.mult, None, [GTt[ktok], Gt], [GTt[kp]])
                        ts(g4(kb, h), g4(ktok, h), G[:, 16 + h:17 + h], None, ALU.mult, None, [GTt[ktok], Gt], [GTt[kb]], eng="pool")
                        ts(g4(qp, h), g4(qtok, h), G[:, 28 + h:29 + h], None, ALU.mult, None, [GTt[qtok], Gt], [GTt[qp]])
                        ts(g4(rv, h), g4(vtok, h), G[:, 16 + h:17 + h], None, ALU.mult, None, [GTt[vtok], Gt], [GTt[rv]], eng="pool")
                        ts(g4(kh, h), g4(ktok, h), G[:, 44 + h:45 + h], None, ALU.mult, None, [GTt[ktok], Gt], [GTt[kh]])
                    for src, dst in ((ktok, kT), (kb, kbT), (qtok, qT), (qp, qpT)):
                        ps, pt = psum.next()
                        for h in range(4):
                            tr(ps[:, h * 128:(h + 1) * 128], g4(src, h), [GTt[src]], [pt])
                        cp(GT[dst][:], ps[:], [pt], [GTt[dst]], eng="act")
                    ps, pt = psum.next()
                    for h in range(4):
                        mm(ps[:, h * 128:(h + 1) * 128], g4(kbT, h), g4(kT, h), True, True, [GTt[kbT], GTt[kT]], [pt])
                    tt(GT[X][:], ps[:], GT[Dm][:], ALU.mult, [pt, GTt[Dm]], [GTt[X]])
                    ps, pt = psum.next()
                    for h in range(4):
                        mm(ps[:, h * 128:(h + 1) * 128], g4(kT, h), g4(kbT, h), True, True, [GTt[kbT], GTt[kT]], [pt])
                    tt(GT[XT][:], ps[:], GT[DTs][:], ALU.mult, [pt, GTt[DTs]], [GTt[XT]])
                    tt(GT[RT][:], I4[:], GT[XT][:], ALU.subtract, [t_const, GTt[XT]], [GTt[RT]], eng="pool")
                    xa, xta, xb_, xtb = X, XT, X2, XT2
                    for step in range(6):
                        ps, pt = psum.next()
                        for h in range(4):
                            mm(ps[:, h * 128:(h + 1) * 128], g4(xta, h), g4(xa, h), True, True, [GTt[xa], GTt[xta]], [pt])
                        cp(GT[xb_][:], ps[:], [pt], [GTt[xb_]], eng="act")
                        if step < 5:
                            ps2, pt2 = psum.next()
                            for h in range(4):
                                mm(ps2[:, h * 128:(h + 1) * 128], g4(xa, h), g4(xta, h), True, True, [GTt[xa], GTt[xta]], [pt2])
                            cp(GT[xtb][:], ps2[:], [pt2], [GTt[xtb]])
                        ps3, pt3 = psum.next()
                        for h in range(4):
                            mm(ps3[:, h * 128:(h + 1) * 128], g4(xb_, h), g4(RT, h), True, True, [GTt[xb_], GTt[RT]], [pt3])
                        tt(GT[RT][:], GT[RT][:], ps3[:], ALU.add, [GTt[RT], pt3], [GTt[RT]])
                        xa, xta, xb_, xtb = xb_, xtb, xa, xta
                    ps, pt = psum.next()
                    for h in range(4):
                        mm(ps[:, h * 128:(h + 1) * 128], g4(kT, h), g4(qT, h), True, True, [GTt[kT], GTt[qT]], [pt])
                    tt(GT[AT][:], ps[:], GT[DTi][:], ALU.mult, [pt, GTt[DTi]], [GTt[AT]])
                    ps, pt = psum.next()
                    for h in range(4):
                        mm(ps[:, h * 128:(h + 1) * 128], g4(RT, h), g4(rv, h), True, True, [GTt[RT], GTt[rv]], [pt])
                    cp(GT[solv][:], ps[:], [pt], [GTt[solv]], eng="act")
                    ps, pt = psum.next()
                    for h in range(4):
                        mm(ps[:, h * 128:(h + 1) * 128], g4(RT, h), g4(kp, h), True, True, [GTt[RT], GTt[kp]], [pt])
                    cp(GT[solk][:], ps[:], [pt], [GTt[solk]])
                    ps, pt = psum.next()
                    for h in range(4):
                        mm(ps[:, h * 128:(h + 1) * 128], g4(solk, h), g4(AT, h), True, True, [GTt[solk], GTt[AT]], [pt])
                    tt(GT[PT][:], GT[qpT][:], ps[:], ALU.subtract, [GTt[qpT], pt], [GTt[PT]])
                    ps, pt = psum.next()
                    for h in range(4):
                        mm(ps[:, h * 128:(h + 1) * 128], g4(solv, h), g4(AT, h), True, False, [GTt[solv], GTt[AT]], [pt])
                        mm(ps[:, h * 128:(h + 1) * 128], g4(Sg, h), g4(PT, h), False, True, [GTt[Sg], GTt[PT]], [pt])
                    cp(GT[obg][:], ps[:], [pt], [GTt[obg]], eng="act")
                    dma(oT[d, 1024:1536, c0:c0 + 128].rearrange("(h p) t -> p h t", p=128),
                        GT[obg][:].rearrange("p (h t) -> p h t", h=4), [GTt[obg]], [tk(t_o, (d, 2, c0 // 512))])
                    ps, pt = psum.next()
                    for h in range(4):
                        mm(ps[:, h * 128:(h + 1) * 128], g4(solk, h), g4(kh, h), True, True, [GTt[solk], GTt[kh]], [pt])
                    for h in range(4):
                        stt(g4(McT, h), ident[:], G[:, 36 + h:37 + h], ps[:, h * 128:(h + 1) * 128], ALU.mult, ALU.subtract,
                            [pt, Gt, t_const], [GTt[McT]])
                    ps, pt = psum.next()
                    for h in range(4):
                        mm(ps[:, h * 128:(h + 1) * 128], g4(kh, h), g4(solv, h), True, False, [GTt[kh], GTt[solv]], [pt])
                        mm(ps[:, h * 128:(h + 1) * 128], g4(McT, h), g4(Sg, h), False, True, [GTt[McT], GTt[Sg]], [pt])
                    cp(GT[Sg][:], ps[:], [pt], [GTt[Sg]])
                if kind == "p":
                    dma(nsd[sq, l, d].rearrange("h k e -> k h e"), GT[Sg][:].rearrange("k (h e) -> k h e", h=4),
                        [GTt[Sg]], [t_out])

    one_col = sb("one_col", [128, 1])
    memset(one_col[:], 1.0, [t_const])

    def recurrent_stub(l, mixes=(0, 1, 2)):
        for r in [m * 4 + h for m in mixes for h in range(4)]:
            for g in range(NTG):
                dma(ybrT[r * 128:(r + 1) * 128, g * 512:(g + 1) * 512], zeros_bf[:], [t_const], [tk(t_ybr, (r, g))])
        for s in range(2):
            for d in range(2):
                for h in range(4):
                    if 0 in mixes:
                        dma(nsg[s, l, d, h], zeros[:, 0:128], [t_const], [t_out])
                    if 1 in mixes:
                        dma(nsc[s, l, d, h], zeros[:, 0:128], [t_const], [t_out])
                    if 2 in mixes:
                        dma(nsd[s, l, d, h], zeros[:, 0:128], [t_const], [t_out])
                if 1 in mixes:
                    dma(nsn[s, l, d], zeros[0:4, 0:128], [t_const], [t_out])
            if 1 in mixes:
                dma(nsm[s, l], zeros[0:2, 0:4], [t_const], [t_out])

    ybr_sb = arena[:, 16384:16384 + 8192].rearrange("p (k c) -> p k c", k=KC)
    t_ybrsb = Tok("ybrsb")
    wbr = Ring([arena[:, 24576 + i * 2048:24576 + (i + 1) * 2048].rearrange("p (n k c) -> p n k c", n=4, k=4) for i in range(2)])
    macc = sb("macc", [128, 512])
    t_macc = Tok("macc")

    def merge_out(l):
        for g in range(NTG):
            dma(ybr_sb, ybrT[:, g * 512:(g + 1) * 512].rearrange("(j p) t -> p j t", p=128),
                [tk(t_ybr, (r, g)) for r in range(16)], [t_ybrsb])
            for m in range(KC):
                wb, wbt = wbr.next()
                P.dma("pool", (lambda wb=wb, m=m: lambda e: e.dma_start(
                    out=wb, in_=w_branch[l][:, :, m * 128:(m + 1) * 128].rearrange("n (kc p) c -> p n kc c", p=128)))(),
                    [], [wbt])
                for n in range(4):
                    ps, pt = psum.next()
                    for kc in range(4):
                        mm(ps[:], wb[:, n, kc, :], ybr_sb[:, n * 4 + kc, :], kc == 0, kc == 3, [wbt, t_ybrsb], [pt])
                    zb, zbt = xblk.next()
                    r0 = C_MERGE + n * D + m * 128
                    dma(zb[:], zT[r0:r0 + 128, g * 512:(g + 1) * 512], ztoks(r0, r0 + 128, g * 512, (g + 1) * 512), [zbt])
                    act(zb[:], zb[:], AF.Sigmoid, [zbt], [zbt])
                    if n == 0:
                        tt(macc[:], ps[:], zb[:], ALU.mult, [pt, zbt], [t_macc])
                    else:
                        tt(zb[:], ps[:], zb[:], ALU.mult, [pt, zbt], [zbt])
                        if n < 3:
                            tt(macc[:], macc[:], zb[:], ALU.add, [t_macc, zbt], [t_macc])
                        else:
                            tt(hT[:, m, g * 512:(g + 1) * 512], macc[:], zb[:], ALU.add, [t_macc, zbt], [hT_tok[g]])
        for blk in range(D // 256):
            wt, wtok = wring.next()
            P.dma("pool", (lambda wt=wt, blk=blk: lambda e: e.dma_start(
                out=wt, in_=w_out[l][:, blk * 256:(blk + 1) * 256].rearrange("(kc p) c -> p kc c", p=128)))(), [], [wtok])
            for g in range(NTG):
                for mb in range(2):
                    ps, pt = psum.next()
                    for kc in range(KC):
                        mm(ps[:], wt[:, kc, mb * 128:(mb + 1) * 128], hT[:, kc, g * 512:(g + 1) * 512],
                           kc == 0, kc == KC - 1, [wtok, hT_tok[g]], [pt])
                    resid_update(ps[:], pt, blk * 2 + mb, g, 5 * 16)

    def final():
        for g in range(NTG):
            pss, pst = psum.next()
            for kc in range(KC):
                xb, xbt = xblk.next()
                dma(xb[:], xT[kc * 128:(kc + 1) * 128, g * 512:(g + 1) * 512], [t_xT[g]], [xbt])
                sq, sqt = tmpb.next()
                act(sq[:], xb[:], AF.Square, [xbt], [sqt])
                mm(pss[:], ones_bf[:], sq[:], kc == 0, kc == KC - 1, [sqt, t_const], [pst])
            act(rstd[:], pss[:], AF.Sqrt, [pst], [t_rstd], bias=eps_col[:], scale=1.0 / D)
            recip(rstd[:], rstd[:], [t_rstd], [t_rstd])
            for kc in range(KC):
                xb, xbt = xblk.next()
                dma(xb[:], xT[kc * 128:(kc + 1) * 128, g * 512:(g + 1) * 512], [t_xT[g]], [xbt])
                stt(xb[:], xb[:], fnw[:, kc:kc + 1], rstd[:], ALU.mult, ALU.mult, [xbt, t_rstd, t_const], [xbt])
                ps, pt = psum.next()
                for j in range(4):
                    tr(ps[:, j * 128:(j + 1) * 128], xb[:, j * 128:(j + 1) * 128], [xbt], [pt])
                o, ot = tmpf.next()
                cp(o[:], ps[:], [pt], [ot], eng="act" if kc % 2 else "dve")
                dma(y_out[g * 512:(g + 1) * 512, kc * 128:(kc + 1) * 128].rearrange("(j t) f -> t j f", t=128),
                    o[:].rearrange("t (j f) -> t j f", j=4), [ot], [t_out])

    import os
    PH = os.environ.get("KPH", "afprtmg")
    for l in range(L):
        if "a" in PH:
            adaln(l)
        if "f" in PH:
            ffn(l, 0, 0)
        if "p" in PH:
            project(l)
        P.barrier()
        MIX = os.environ.get("KMIX", "012")
        if "r" in PH:
            recurrent_stub(l, tuple(m for m in (0, 1, 2) if str(m) not in MIX))
            if "0" in MIX:
                gla(l)
                post(l, 0, C_GLA_G, AF.Silu)
            if "1" in MIX:
                mlstm(l)
                post(l, 1, C_ML_O, AF.Sigmoid)
            if "2" in MIX:
                P.barrier()
                gdn_conv(l)
                P.barrier()
                gdn(l)
                P.barrier()
                post(l, 2, C_GD_G, AF.Silu)
        P.barrier()
        if "t" in PH:
            attention(l)
        P.barrier()
        if "m" in PH:
            merge_out(l)
        if "g" in PH:
            ffn(l, 1, 2)
    final()
    P.emit()
    st.close()
    return nc


def _rope_tables(TS, grid_w=64):
    rows = TS // grid_w
    row = np.repeat(np.arange(rows), grid_w).astype(np.float32)
    col = (np.arange(rows * grid_w) % grid_w).astype(np.float32)
    n_pairs = HD // 4
    inv = (10000.0 ** (-np.arange(n_pairs, dtype=np.float32) / n_pairs)).astype(np.float32)
    ang = np.concatenate([row[:, None] * inv, col[:, None] * inv], axis=-1)
    cos = np.repeat(np.cos(ang), 2, axis=1).T.astype(np.float32)
    sin = np.repeat(np.sin(ang), 2, axis=1).T.astype(np.float32)
    return np.ascontiguousarray(cos), np.ascontiguousarray(sin)


def _rmat():
    R = np.zeros((128, 128), np.float32)
    for i in range(64):
        R[2 * i + 1, 2 * i] = -1.0
        R[2 * i, 2 * i + 1] = 1.0
    return R


def make_in_maps(inp, L, TS, n_cores=8):
    f = lambda a: np.ascontiguousarray(np.asarray(a, dtype=np.float32))
    cos, sin = _rope_tables(TS)
    shared = dict(
        w_ada=f(inp["w_ada"][:L]), w_ffn_in=f(inp["w_ffn_in"][:L]), w_ffn_out=f(inp["w_ffn_out"][:L]),
        w_in=f(inp["w_in"][:L]), w_branch=f(inp["w_branch"][:L]), w_out=f(inp["w_out"][:L]),
        b_adaT=f(np.asarray(inp["b_ada"][:L]).reshape(L, 144, 128).transpose(0, 2, 1)),
        fnwT=f(np.asarray(inp["final_norm_w"]).reshape(KC, 128).T),
        qkw=f(np.stack([np.asarray(inp["q_norm_w"][:L]).T, np.asarray(inp["k_norm_w"][:L]).T], axis=-1)),
        rope_cos=cos, rope_sin=sin, rmat=_rmat(), ident=np.eye(128, dtype=np.float32),
        tri=np.stack([np.triu(np.ones((128, 128), np.float32)), np.tril(np.ones((128, 128), np.float32))]),
        gla_w2aug=f(np.concatenate([np.asarray(inp["gla_w2"][:L]), np.zeros((L, 2, 16, 512), np.float32),
                                    np.asarray(inp["gla_b2"][:L])[:, :, None, :]], axis=2)),
        gd_cwT=f(np.asarray(inp["gd_conv_w"][:L]).reshape(L, 5, 12, 128).transpose(0, 3, 2, 1)),
        gd_gb=f(np.broadcast_to(np.concatenate([np.asarray(inp["gd_dt_bias"][:L]), np.asarray(inp["gd_a_log"][:L])], axis=-1)[None],
                                (128, L, 2, 8))),
        strm=np.stack([np.triu(np.ones((128, 128), np.float32), 1), np.tril(np.ones((128, 128), np.float32), -1)]),
        ml_gb=f(np.broadcast_to(np.asarray(inp["ml_gate_b"][:L]).reshape(L, 2, 8)[None], (128, L, 2, 8))),
        normw=f(np.stack([np.asarray(inp["gla_norm_w"][:L]).T, np.asarray(inp["ml_norm_w"][:L]).T,
                          np.asarray(inp["gd_norm_w"][:L]).T], axis=-1)),
    )
    xp = np.asarray(inp["x_prompt"], np.float32)
    xs = np.asarray(inp["x_sample"], np.float32)
    maps = []
    for c in range(n_cores):
        b = min(c // 4, xs.shape[0] - 1)
        m = dict(shared)
        m["xin"] = np.ascontiguousarray(np.concatenate([xp[2 * c].reshape(256, D), xp[2 * c + 1].reshape(256, D),
                                                         xs[b, :TS]], axis=0))
        cc = np.stack([np.asarray(inp["c_ctx"], np.float32), np.asarray(inp["c"], np.float32)[b]], axis=-1)
        m["cT"] = np.ascontiguousarray(cc.reshape(KC, 128, 2).transpose(1, 0, 2))
        m["cache_k"] = f(inp["cache_k"][b, :L])
        m["cache_v"] = f(inp["cache_v"][b, :L])
        m["state_gla"] = f(inp["state_gla"][b, :L])
        m["state_mc"] = f(inp["state_mlstm_c"][b, :L])
        m["state_gd"] = f(inp["state_gdn"][b, :L])
        m["ml_n0T"] = f(np.asarray(inp["state_mlstm_n"][b, :L]).transpose(0, 1, 3, 2))
        m["ml_m0b"] = f(np.broadcast_to(np.asarray(inp["state_mlstm_m"][b, :L])[None], (128, L, 2, 4)))
        maps.append(m)
    return maps


_NC_CACHE = {}


def run_cfg(inp, L, TS, n_cores=8):
    key = (L, TS)
    if key not in _NC_CACHE:
        _NC_CACHE[key] = build(L, TS)
    nc = _NC_CACHE[key]
    maps = make_in_maps(inp, L, TS, n_cores)
    res = run_bass_kernel_spmd(nc, maps, core_ids=list(range(n_cores)))
    return res.results


def kernel(**inputs):
    L, TS = 4, 2048
    r = run_cfg(inputs, L, TS)
    y_prompt = np.stack([r[c]["y"][s * 256:(s + 1) * 256] for c in range(8) for s in range(2)], axis=0)
    y_sample = np.stack([r[0]["y"][512:], r[4]["y"][512:]], axis=0)
    cat = lambda k: np.concatenate([r[c][k] for c in range(8)], axis=0)
    return (y_prompt.astype(np.float32), y_sample.astype(np.float32), cat("nck"), cat("ncv"), cat("nsg"),
            cat("nsc"), cat("nsn"), cat("nsm"), cat("nsd"))
```

```python
import contextlib
import math
import numpy as np
import concourse.bass as bass
import concourse.mybir as mybir
from concourse.bass_utils import run_bass_kernel_spmd

F32 = mybir.dt.float32
BF16 = mybir.dt.bfloat16
AF = mybir.ActivationFunctionType
ALU = mybir.AluOpType

D = 2048
KC = 16
DFF = 5632
HC = 44
INW = 15424
EPS = 1e-6
HD = 128
C_GLA_Q, C_GLA_K, C_GLA_V, C_GLA_G, C_GLA_LR = 0, 512, 1024, 1536, 2048
C_ML_Q, C_ML_K, C_ML_V, C_ML_O, C_ML_IF = 2080, 2592, 3104, 3616, 4128
C_GD_QKV, C_GD_G, C_GD_AB = 4144, 5680, 6192
C_AT_Q, C_AT_K, C_AT_V, C_MERGE = 6208, 6720, 6976, 7232


class Tok:
    __slots__ = ("name", "last_w", "readers", "psum")

    def __init__(self, name="", psum=False):
        self.name = name
        self.last_w = None
        self.readers = []
        self.psum = psum


class Op:
    __slots__ = ("eng", "fn", "deps", "needed", "val", "dma_sem", "is_dma")

    def __init__(self, eng, fn, is_dma=False):
        self.eng = eng
        self.fn = fn
        self.deps = []
        self.needed = False
        self.val = None
        self.dma_sem = None
        self.is_dma = is_dma


class Prog:
    ENGS = ("pe", "act", "dve", "pool", "sp")

    def __init__(self, nc, n_dma_sems=48):
        self.nc = nc
        self.ops = {e: [] for e in self.ENGS}
        self.n_dma_sems = n_dma_sems
        self.dma_rr = 0
        self.dma_rr_sw = 0
        self.dma_last = [None] * n_dma_sems
        self.dma_cnt = [0] * n_dma_sems
        self.all_ops = []

    def _deps(self, op, reads, writes):
        for t in reads:
            if t.last_w is not None:
                op.deps.append(t.last_w)
            if t.psum:
                assert all(r.eng == op.eng for r in t.readers), "two engines read PSUM tile " + t.name
        for t in writes:
            if t.last_w is not None:
                op.deps.append(t.last_w)
            op.deps.extend(t.readers)
        for t in reads:
            t.readers.append(op)
        for t in writes:
            t.last_w = op
            t.readers = []

    def op(self, eng, fn, reads=(), writes=()):
        o = Op(eng, fn)
        self._deps(o, reads, writes)
        self.ops[eng].append(o)
        self.all_ops.append(o)
        return o

    def dma(self, eng, fn, reads=(), writes=()):
        o = Op(eng, fn, is_dma=True)
        if eng == "pool":
            j = self.dma_rr_sw
            self.dma_rr_sw = (self.dma_rr_sw + 1) % 12
        else:
            j = 12 + self.dma_rr
            self.dma_rr = (self.dma_rr + 1) % (self.n_dma_sems - 12)
        o.dma_sem = j
        if self.dma_last[j] is not None:
            o.deps.append(self.dma_last[j])
        self.dma_cnt[j] += 1
        o.val = 16 * self.dma_cnt[j]
        self.dma_last[j] = o
        o.needed = True
        self._deps(o, reads, writes)
        self.ops[eng].append(o)
        self.all_ops.append(o)
        return o

    def barrier(self):
        last = [self.ops[e][-1] for e in self.ENGS if self.ops[e]]
        last += [d for d in self.dma_last if d is not None]
        for e in self.ENGS:
            o = Op(e, lambda h: h.nop())
            o.deps = list(last)
            self.ops[e].append(o)
            self.all_ops.append(o)

    def emit(self):
        nc = self.nc
        for o in self.all_ops:
            for d in o.deps:
                d.needed = True
        for e in self.ENGS:
            c = 0
            for o in self.ops[e]:
                if o.is_dma:
                    continue
                if o.needed:
                    c += 1
                    o.val = c
        with contextlib.ExitStack() as st:
            esem = {e: st.enter_context(nc.semaphore("s_" + e)) for e in self.ENGS}
            dsem = [st.enter_context(nc.semaphore("d%d" % j)) for j in range(self.n_dma_sems)]
            block = st.enter_context(nc.Block())

            def run(ename, h):
                known = {}
                for o in self.ops[ename]:
                    need = {}
                    for d in o.deps:
                        k = ("d", d.dma_sem) if d.is_dma else ("e", d.eng)
                        if (not d.is_dma) and d.eng == ename and ename == "pe":
                            continue
                        if known.get(k, 0) >= d.val:
                            continue
                        if need.get(k, 0) < d.val:
                            need[k] = d.val
                    for k, v in need.items():
                        s = dsem[k[1]] if k[0] == "d" else esem[k[1]]
                        h.wait_ge(s, v)
                        known[k] = v
                    ins = o.fn(h)
                    if o.is_dma:
                        ins.then_inc(dsem[o.dma_sem], 16)
                    elif o.needed:
                        ins.then_inc(esem[ename], 1)
                last = {}
                for o in self.ops[ename]:
                    if o.is_dma:
                        last[o.dma_sem] = max(last.get(o.dma_sem, 0), o.val)
                for j, v in last.items():
                    if known.get(("d", j), 0) < v:
                        h.wait_ge(dsem[j], v)

            @block.tensor
            def _(h):
                run("pe", h)

            @block.scalar
            def _(h):
                run("act", h)

            @block.vector
            def _(h):
                run("dve", h)

            @block.gpsimd
            def _(h):
                run("pool", h)

            @block.sync
            def _(h):
                run("sp", h)


class Ring:
    def __init__(self, bufs, toks=None):
        self.bufs = bufs
        self.toks = toks if toks is not None else [Tok() for _ in bufs]
        self.i = 0

    def next(self):
        b, t = self.bufs[self.i], self.toks[self.i]
        self.i = (self.i + 1) % len(self.bufs)
        return b, t


def build(L, TS, full_w_layers=None):
    NTG = 1 + TS // 512
    NT = NTG * 512
    NSQ = TS // 512
    nc = bass.Bass("TRN2", target_bir_lowering=False)
    st = contextlib.ExitStack()

    def din(name, shape, dt=F32):
        return nc.dram_tensor(name, list(shape), dt, kind="ExternalInput").ap()

    def dout(name, shape, dt=F32):
        return nc.dram_tensor(name, list(shape), dt, kind="ExternalOutput").ap()

    def dscr(name, shape, dt=F32):
        return nc.dram_tensor(name, list(shape), dt, kind="Internal").ap()

    xin = din("xin", [NT, D])
    cT = din("cT", [128, KC, 2])
    w_ada = din("w_ada", [L, D, 9 * D])
    b_adaT = din("b_adaT", [L, 128, 144])
    w_ffn_in = din("w_ffn_in", [L, 2, D, 2 * DFF])
    w_ffn_out = din("w_ffn_out", [L, 2, DFF, D])
    w_in = din("w_in", [L, D, INW])
    w_branch = din("w_branch", [L, 4, 512, D])
    w_out = din("w_out", [L, D, D])
    fnwT = din("fnwT", [128, KC])
    qkw = din("qkw", [128, L, 2])
    cache_k = din("cache_k", [L, 256, 2, 128])
    cache_v = din("cache_v", [L, 256, 2, 128])
    rope_cos = din("rope_cos", [128, TS])
    rope_sin = din("rope_sin", [128, TS])
    rmat = din("rmat", [128, 128])
    tri_in = din("tri", [2, 128, 128])
    gla_w2aug = din("gla_w2aug", [L, 2, 33, 512])
    state_gla = din("state_gla", [L, 2, 4, 128, 128])
    normw = din("normw", [128, L, 3])
    state_mc = din("state_mc", [L, 2, 4, 128, 128])
    state_gd = din("state_gd", [L, 2, 4, 128, 128])
    gd_cwT = din("gd_cwT", [L, 128, 12, 5])
    gd_gb = din("gd_gb", [128, L, 2, 8])
    strm_in = din("strm", [2, 128, 128])
    ml_n0T = din("ml_n0T", [L, 2, 128, 4])
    ml_m0b = din("ml_m0b", [128, L, 2, 4])
    ml_gb = din("ml_gb", [128, L, 2, 8])
    ident_in = din("ident", [128, 128])

    y_out = dout("y", [NT, D])
    nck = dout("nck", [2, L, 256, 2, 128])
    ncv = dout("ncv", [2, L, 256, 2, 128])
    nsg = dout("nsg", [2, L, 2, 4, 128, 128])
    nsc = dout("nsc", [2, L, 2, 4, 128, 128])
    nsn = dout("nsn", [2, L, 2, 4, 128])
    nsm = dout("nsm", [2, L, 2, 4])
    nsd = dout("nsd", [2, L, 2, 4, 128, 128])

    xT = dscr("xT", [D, NT])
    hidT = dscr("hidT", [DFF, NT], BF16)
    zT = dscr("zT", [INW, NT])
    ybrT = dscr("ybrT", [D, NT], BF16)
    oT = dscr("oT", [2, 1536, NT])
    gqkv = dscr("gqkv", [NT, 1536])
    w2bf = dscr("w2bf", [KC, 128, HC * 128], BF16)
    t_w2bf = {}

    def sb(name, shape, dt=F32):
        return st.enter_context(nc.sbuf_tensor(name, list(shape), dt))

    P = Prog(nc)

    hT = sb("hT", [128, KC, NT], BF16)
    hT_tok = [Tok("hT%d" % g) for g in range(NTG)]
    modT = sb("modT", [128, 144, 2])
    t_mod = Tok("mod")
    scT = sb("scT", [128, KC, 2], BF16)
    t_sc = Tok("sc")
    ident = sb("ident_sb", [128, 128])
    ones_bf = sb("ones_bf", [128, 128], BF16)
    rmat_bf = sb("rmat_bf", [128, 128], BF16)
    fnw = sb("fnw", [128, KC])
    qkw_sb = sb("qkw_sb", [128, L, 2])
    badd = sb("badd", [128, 144])
    t_const = Tok("const")
    zeros = sb("zeros", [128, 128])
    zeros_bf = sb("zeros_bf", [128, 512], BF16)

    _banks = [st.enter_context(nc.psum_tensor("ps%d" % i, [128, 512], F32)) for i in range(8)]
    psum = Ring(_banks[0:6])
    pacc = Ring(_banks[6:8])
    for _r in (psum, pacc):
        for _t in _r.toks:
            _t.psum = True
    arena = sb("arena_bf", [128, 33792], BF16)
    wring = Ring([arena[:, i * 4096:(i + 1) * 4096].rearrange("p (k c) -> p k c", k=KC) for i in range(4)])
    w2ring = Ring([arena[:, 22528 + i * 5632:22528 + (i + 1) * 5632].rearrange("p (k c) -> p k c", k=HC) for i in range(2)])
    hidin = arena[:, 0:22528].rearrange("p (k c) -> p k c", k=HC)
    t_hidin = Tok("hidin")
    xblk = Ring([sb("xb%d" % i, [128, 512]) for i in range(3)])
    tmpf = Ring([sb("tf%d" % i, [128, 512]) for i in range(3)])
    tmpb = Ring([sb("tb%d" % i, [128, 512], BF16) for i in range(3)])
    rstd = sb("rstd", [128, 512])
    t_rstd = Tok("rstd")

    t_xT = [Tok("xT%d" % g) for g in range(NTG)]
    t_hid = {}
    t_z = {}
    t_ybr = {}
    t_out = Tok("out")

    def tk(d, key):
        if key not in d:
            d[key] = Tok(str(key))
        return d[key]

    def ztoks(r0, r1, c0, c1):
        out = []
        for rb in range(r0 // 128, (r1 - 1) // 128 + 1):
            for g in range(c0 // 512, (c1 - 1) // 512 + 1):
                out.append(tk(t_z, (rb, g)))
        return out

    def mm(out, lhsT, rhs, start, stop, reads, writes):
        P.op("pe", lambda e: e.matmul(out, lhsT=lhsT, rhs=rhs, start=start, stop=stop), reads, writes)

    def tr(out, in_, reads, writes):
        kp = in_.shape[0]
        P.op("pe", lambda e: e.transpose(out, in_, ident[0:kp, 0:kp]), list(reads) + [t_const], writes)

    def act(out, in_, func, reads, writes, bias=None, scale=None):
        kw = {}
        if bias is not None:
            kw["bias"] = bias
        if scale is not None:
            kw["scale"] = scale
        P.op("act", lambda e: e.activation(out=out, in_=in_, func=func, **kw), reads, writes)

    def tt(out, in0, in1, op, reads, writes, eng="dve"):
        P.op(eng, lambda e: e.tensor_tensor(out=out, in0=in0, in1=in1, op=op), reads, writes)

    def ts(out, in0, s1, s2, op0, op1, reads, writes, eng="dve"):
        if op1 is None:
            P.op(eng, lambda e: e.tensor_scalar(out=out, in0=in0, scalar1=s1, scalar2=None, op0=op0), reads, writes)
        else:
            P.op(eng, lambda e: e.tensor_scalar(out=out, in0=in0, scalar1=s1, scalar2=s2, op0=op0, op1=op1), reads, writes)

    def stt(out, in0, scalar, in1, op0, op1, reads, writes):
        P.op("dve", lambda e: e.scalar_tensor_tensor(out=out, in0=in0, scalar=scalar, in1=in1, op0=op0, op1=op1), reads, writes)

    def cp(out, in_, reads, writes, eng="dve"):
        if eng == "act":
            P.op(eng, lambda e: e.activation(out=out, in_=in_, func=AF.Copy), reads, writes)
        else:
            P.op(eng, lambda e: e.tensor_copy(out=out, in_=in_), reads, writes)

    def recip(out, in_, reads, writes):
        P.op("dve", lambda e: e.reciprocal(out=out, in_=in_), reads, writes)

    def dma(out, in_, reads, writes, eng="sp"):
        P.dma(eng, lambda e: e.dma_start(out=out, in_=in_), reads, writes)

    def memset(ap, val, writes, eng="dve"):
        P.op(eng, lambda e: e.memset(ap, val), (), writes)

    dma(ident[:], ident_in, [], [t_const])
    dma(fnw[:], fnwT, [], [t_const])
    dma(qkw_sb[:], qkw, [], [t_const])
    P.dma("pool", lambda e: e.dma_start(out=rmat_bf[:], in_=rmat), [], [t_const])
    memset(ones_bf[:], 1.0, [t_const])
    memset(zeros[:], 0.0, [t_const])
    memset(zeros_bf[:], 0.0, [t_const])

    for g in range(NTG):
        for tb in range(4):
            xt, xtok = xblk.next()
            r0 = g * 512 + tb * 128
            for fq in range(4):
                xt, xtok = xblk.next()
                dma(xt[:], xin[r0:r0 + 128, fq * 512:(fq + 1) * 512], [], [xtok])
                ps, pt = psum.next()
                for j in range(4):
                    tr(ps[:, j * 128:(j + 1) * 128], xt[:, j * 128:(j + 1) * 128], [xtok], [pt])
                o, otok = tmpf.next()
                cp(o[:], ps[:], [pt], [otok], eng="act" if (tb + fq) % 2 else "dve")
                dma(xT[fq * 512:(fq + 1) * 512, r0:r0 + 128].rearrange("(j p) t -> p j t", p=128),
                    o[:].rearrange("p (j t) -> p j t", j=4), [otok], [t_xT[g]])

    def cond_of(g):
        return 0 if g == 0 else 1

    def rms_mod(sub):
        for g in range(NTG):
            c = cond_of(g)
            pss, pst = psum.next()
            for kc in range(KC):
                xb, xbt = xblk.next()
                dma(xb[:], xT[kc * 128:(kc + 1) * 128, g * 512:(g + 1) * 512], [t_xT[g]], [xbt])
                sq, sqt = tmpb.next()
                act(sq[:], xb[:], AF.Square, [xbt], [sqt])
                mm(pss[:], ones_bf[:], sq[:], kc == 0, kc == KC - 1, [sqt, t_const], [pst])
            act(rstd[:], pss[:], AF.Sqrt, [pst], [t_rstd], bias=eps_col[:], scale=1.0 / D)
            recip(rstd[:], rstd[:], [t_rstd], [t_rstd])
            for kc in range(KC):
                xb, xbt = xblk.next()
                dma(xb[:], xT[kc * 128:(kc + 1) * 128, g * 512:(g + 1) * 512], [t_xT[g]], [xbt])
                tf, tft = tmpf.next()
                tt(tf[:], xb[:], rstd[:], ALU.mult, [xbt, t_rstd], [tft])
                ts(hT[:, kc, g * 512:(g + 1) * 512], tf[:], modT[:, (3 * sub + 1) * 16 + kc, c:c + 1],
                   modT[:, (3 * sub) * 16 + kc, c:c + 1], ALU.mult, ALU.add, [tft, t_mod], [hT_tok[g]],
                   eng="pool" if kc % 2 else "dve")

    eps_col = sb("eps_col", [128, 1])
    memset(eps_col[:], EPS, [t_const])

    def resid_update(ps_ap, pt, m, g, gate_chunk0, ncols=512, c0=0):
        c = cond_of(g)
        xb, xbt = xblk.next()
        col0 = g * 512 + c0
        dma(xb[:, :ncols], xT[m * 128:(m + 1) * 128, col0:col0 + ncols], [t_xT[g]], [xbt])
        stt(xb[:, :ncols], ps_ap, modT[:, gate_chunk0 + m, c:c + 1], xb[:, :ncols], ALU.mult, ALU.add,
            [pt, xbt, t_mod], [xbt])
        dma(xT[m * 128:(m + 1) * 128, col0:col0 + ncols], xb[:, :ncols], [xbt], [t_xT[g]])

    def adaln(l):
        dma(badd[:], b_adaT[l], [], [t_mod])
        ctmp, ctt = tmpf.next()
        dma(ctmp[:, 0:KC * 2], cT.rearrange("p k c -> p (k c)"), [], [ctt])
        act(scT[:].rearrange("p k c -> p (k c)"), ctmp[:, 0:KC * 2], AF.Silu, [ctt], [t_sc])
        for blk in range(9 * D // 256):
            wt, wtok = wring.next()
            P.dma("pool", (lambda wt=wt, blk=blk: lambda e: e.dma_start(
                out=wt, in_=w_ada[l][:, blk * 256:(blk + 1) * 256].rearrange("(kc p) c -> p kc c", p=128)))(),
                [], [wtok])
            for mb in range(2):
                ps, pt = psum.next()
                for kc in range(KC):
                    mm(ps[:, 0:2], wt[:, kc, mb * 128:(mb + 1) * 128], scT[:, kc, :], kc == 0, kc == KC - 1,
                       [wtok, t_sc], [pt])
                j = blk * 2 + mb
                ts(modT[:, j, :], ps[:, 0:2], badd[:, j:j + 1], None, ALU.add, None, [pt, t_mod], [t_mod])
        for sub in range(3):
            ch = (3 * sub + 1) * 16
            ts(modT[:, ch:ch + 16, :], modT[:, ch:ch + 16, :], 1.0, None, ALU.add, None, [t_mod], [t_mod])
        for sub in (0, 2):
            ch = (3 * sub + 2) * 16
            ts(modT[:, ch:ch + 16, :], modT[:, ch:ch + 16, :], 0.5, None, ALU.mult, None, [t_mod], [t_mod])

    def ffn(l, f, sub):
        rms_mod(sub)
        W1 = w_ffn_in[l, f]
        for jb in range(DFF // 256):
            wg, wgt = wring.next()
            wu, wut = wring.next()
            P.dma("pool", (lambda wg=wg, jb=jb: lambda e: e.dma_start(
                out=wg, in_=W1[:, jb * 256:(jb + 1) * 256].rearrange("(kc p) c -> p kc c", p=128)))(), [], [wgt])
            P.dma("pool", (lambda wu=wu, jb=jb: lambda e: e.dma_start(
                out=wu, in_=W1[:, DFF + jb * 256:DFF + (jb + 1) * 256].rearrange("(kc p) c -> p kc c", p=128)))(), [], [wut])
            for g in range(NTG):
                for mb in range(2):
                    pg, pgt = psum.next()
                    pu, put = psum.next()
                    for kc in range(KC):
                        mm(pg[:], wg[:, kc, mb * 128:(mb + 1) * 128], hT[:, kc, g * 512:(g + 1) * 512],
                           kc == 0, kc == KC - 1, [wgt, hT_tok[g]], [pgt])
                    for kc in range(KC):
                        mm(pu[:], wu[:, kc, mb * 128:(mb + 1) * 128], hT[:, kc, g * 512:(g + 1) * 512],
                           kc == 0, kc == KC - 1, [wut, hT_tok[g]], [put])
                    sg, sgt = tmpf.next()
                    act(sg[:], pg[:], AF.Silu, [pgt], [sgt])
                    hb, hbt = tmpb.next()
                    tt(hb[:], sg[:], pu[:], ALU.mult, [sgt, put], [hbt])
                    j = jb * 2 + mb
                    dma(hidT[j * 128:(j + 1) * 128, g * 512:(g + 1) * 512], hb[:], [hbt], [tk(t_hid, (j, g))])
        W2 = w_ffn_out[l, f]
        gate0 = (3 * sub + 2) * 16
        P.barrier()
        for g in range(NTG):
            dma(hidin, hidT[:, g * 512:(g + 1) * 512].rearrange("(j p) t -> p j t", p=128),
                [tk(t_hid, (j, g)) for j in range(HC)], [t_hidin])
            for m in range(KC):
                wi_ = w2ring.i
                w2, w2t = w2ring.next()
                w2flat = arena[:, 22528 + wi_ * 5632:22528 + (wi_ + 1) * 5632]
                if g == 0:
                    P.dma("pool", (lambda w2=w2, m=m: lambda e: e.dma_start(
                        out=w2, in_=W2[:, m * 128:(m + 1) * 128].rearrange("(j p) c -> p j c", p=128)))(), [], [w2t])
                    dma(w2bf[m], w2flat, [w2t], [tk(t_w2bf, m)])
                else:
                    dma(w2flat, w2bf[m], [tk(t_w2bf, m)], [w2t], eng="act")
                ps, pt = psum.next()
                for j in range(HC):
                    mm(ps[:], w2[:, j, :], hidin[:, j, :], j == 0, j == HC - 1, [w2t, t_hidin], [pt])
                resid_update(ps[:], pt, m, g, gate0)
        P.barrier()

    def project(l):
        rms_mod(1)
        nblk = (INW + 255) // 256
        for blk in range(nblk):
            c0 = blk * 256
            wd = min(256, INW - c0)
            wt, wtok = wring.next()
            P.dma("pool", (lambda wt=wt, c0=c0, wd=wd: lambda e: e.dma_start(
                out=wt[:, :, 0:wd], in_=w_in[l][:, c0:c0 + wd].rearrange("(kc p) c -> p kc c", p=128)))(), [], [wtok])
            for g in range(NTG):
                for mb in range((wd + 127) // 128):
                    m = min(128, wd - mb * 128)
                    ps, pt = psum.next()
                    for kc in range(KC):
                        mm(ps[:m, :], wt[:, kc, mb * 128:mb * 128 + m], hT[:, kc, g * 512:(g + 1) * 512],
                           kc == 0, kc == KC - 1, [wtok, hT_tok[g]], [pt])
                    o, ot = tmpf.next()
                    cp(o[:m, :], ps[:m, :], [pt], [ot], eng="act" if (mb + g) % 2 else "dve")
                    r0 = c0 + mb * 128
                    dma(zT[r0:r0 + m, g * 512:(g + 1) * 512], o[:m, :], [ot], ztoks(r0, r0 + m, g * 512, (g + 1) * 512))

    NKB = 2 + TS // 128
    kT_all = arena[:, 0:256 + TS]
    v_all = arena[:, 2304:2304 + NKB * 128]
    qT_bf = arena[:, 4608:4608 + TS]
    t_kT, t_v, t_q = Tok("kT"), Tok("v"), Tok("q")

    def headnorm_cols(row0, col0, n, wcol, rope_c0, out_bf, out_tok, out_c0, keep_f32=None):
        xb, xbt = xblk.next()
        dma(xb[:, :n], zT[row0:row0 + 128, col0:col0 + n], ztoks(row0, row0 + 128, col0, col0 + n), [xbt])
        sq, sqt = tmpb.next()
        act(sq[:, :n], xb[:, :n], AF.Square, [xbt], [sqt])
        ps, pt = psum.next()
        mm(ps[:, :n], ones_bf[:], sq[:, :n], True, True, [sqt, t_const], [pt])
        rs, rst = tmpf.next()
        act(rs[:, :n], ps[:, :n], AF.Sqrt, [pt], [rst], bias=eps_col[:], scale=1.0 / HD)
        recip(rs[:, :n], rs[:, :n], [rst], [rst])
        tt(xb[:, :n], xb[:, :n], rs[:, :n], ALU.mult, [xbt, rst], [xbt])
        if rope_c0 is None:
            ts(out_bf[:, out_c0:out_c0 + n], xb[:, :n], wcol, None, ALU.mult, None, [xbt, t_const], [out_tok])
            if keep_f32 is not None:
                ts(keep_f32[0][:, :n], xb[:, :n], wcol, None, ALU.mult, None, [xbt, t_const], [keep_f32[1]])
            return
        ts(xb[:, :n], xb[:, :n], wcol, None, ALU.mult, None, [xbt, t_const], [xbt])
        nb, nbt = tmpb.next()
        cp(nb[:, :n], xb[:, :n], [xbt], [nbt], eng="act")
        ps2, pt2 = psum.next()
        mm(ps2[:, :n], rmat_bf[:], nb[:, :n], True, True, [nbt, t_const], [pt2])
        sn, snt = xblk.next()
        dma(sn[:, :n], rope_sin[:, rope_c0:rope_c0 + n], [], [snt])
        t1, t1t = tmpf.next()
        tt(t1[:, :n], ps2[:, :n], sn[:, :n], ALU.mult, [pt2, snt], [t1t])
        cs, cst = xblk.next()
        dma(cs[:, :n], rope_cos[:, rope_c0:rope_c0 + n], [], [cst])
        tt(xb[:, :n], xb[:, :n], cs[:, :n], ALU.mult, [xbt, cst], [xbt], eng="pool")
        tt(out_bf[:, out_c0:out_c0 + n], xb[:, :n], t1[:, :n], ALU.add, [xbt, t1t], [out_tok])

    def to_tokmajor_bf(row0, col0, nblk, out3, out_tok, blk0, dram_out=None):
        for i in range(0, nblk, 4):
            nb_ = min(4, nblk - i)
            xb, xbt = xblk.next()
            n = nb_ * 128
            dma(xb[:, :n], zT[row0:row0 + 128, col0 + i * 128:col0 + i * 128 + n],
                ztoks(row0, row0 + 128, col0 + i * 128, col0 + i * 128 + n), [xbt])
            ps, pt = psum.next()
            for j in range(nb_):
                tr(ps[:, j * 128:(j + 1) * 128], xb[:, j * 128:(j + 1) * 128], [xbt], [pt])
            if dram_out is not None:
                o, ot = tmpf.next()
                cp(o[:, :n], ps[:, :n], [pt], [ot], eng="act")
                cp(out3[:, (blk0 + i) * 128:(blk0 + i) * 128 + n], o[:, :n], [ot], [out_tok])
                for j in range(nb_):
                    dma(dram_out(i + j), o[:, j * 128:(j + 1) * 128], [ot], [t_out])
            else:
                cp(out3[:, (blk0 + i) * 128:(blk0 + i) * 128 + n], ps[:, :n], [pt], [out_tok])

    def attn_core(h, nq_cols, q_c0, nkb, ycol0):
        po, pot = pacc.next()
        pd, pdt = pacc.next()
        for kb in range(nkb):
            ps, pt = psum.next()
            mm(ps[:, :nq_cols], kT_all[:, kb * 128:(kb + 1) * 128], qT_bf[:, q_c0:q_c0 + nq_cols], True, True,
               [t_kT, t_q], [pt])
            pb, pbt = tmpb.next()
            act(pb[:, :nq_cols], ps[:, :nq_cols], AF.Exp, [pt], [pbt], scale=HD ** -0.5)
            mm(po[:, :nq_cols], v_all[:, kb * 128:(kb + 1) * 128], pb[:, :nq_cols], kb == 0, kb == nkb - 1, [t_v, pbt], [pot])
            mm(pd[:, :nq_cols], ones_bf[:], pb[:, :nq_cols], kb == 0, kb == nkb - 1, [t_const, pbt], [pdt])
        rd, rdt = tmpf.next()
        recip(rd[:, :nq_cols], pd[:, :nq_cols], [pdt], [rdt])
        yb, ybt = tmpb.next()
        tt(yb[:, :nq_cols], po[:, :nq_cols], rd[:, :nq_cols], ALU.mult, [pot, rdt], [ybt])
        r0 = 1536 + h * 128
        dma(ybrT[r0:r0 + 128, ycol0:ycol0 + nq_cols], yb[:, :nq_cols], [ybt],
            [tk(t_ybr, (r0 // 128, ycol0 // 512))])

    def attention(l):
        qcol = qkw_sb[:, l, 0:1]
        kcol = qkw_sb[:, l, 1:2]
        import os
        KAT = os.environ.get("KAT", "pkvcs")
        for s in (range(2) if "p" in KAT else []):
            col0 = s * 256
            for gk in range(2):
                kf, kft = tmpf.next()
                headnorm_cols(C_AT_K + gk * 128, col0, 256, kcol, None, kT_all, t_kT, 0, keep_f32=(kf, kft))
                if "k" in KAT:
                    ps, pt = psum.next()
                    for j in range(2):
                        tr(ps[:, j * 128:(j + 1) * 128], kf[:, j * 128:(j + 1) * 128], [kft], [pt])
                    o, ot = tmpf.next()
                    cp(o[:, :256], ps[:, :256], [pt], [ot], eng="act")
                    for j in range(2):
                        dma(nck[s, l, j * 128:(j + 1) * 128, gk, :], o[:, j * 128:(j + 1) * 128], [ot], [t_out])
                if "v" in KAT:
                    to_tokmajor_bf(C_AT_V + gk * 128, col0, 2, v_all, t_v, 0,
                                   dram_out=lambda i, s=s, gk=gk: ncv[s, l, i * 128:(i + 1) * 128, gk, :])
                for hh in range(2):
                    h = gk * 2 + hh
                    headnorm_cols(C_AT_Q + h * 128, col0, 256, qcol, None, qT_bf, t_q, 0)
                    if "c" in KAT:
                        attn_core(h, 256, 0, 2, col0)
        for gk in (range(2) if "s" in KAT else []):
            ck, ckt = tmpf.next()
            dma(ck[:, :256].rearrange("p (j e) -> p j e", j=2),
                cache_k[l, :, gk, :].rearrange("(j p) e -> p j e", p=128), [], [ckt])
            ps, pt = psum.next()
            for j in range(2):
                tr(ps[:, j * 128:(j + 1) * 128], ck[:, j * 128:(j + 1) * 128], [ckt], [pt])
            cp(kT_all[:, 0:256], ps[:, 0:256], [pt], [t_kT])
            cv, cvt = tmpf.next()
            dma(cv[:, :256].rearrange("p (j e) -> p j e", j=2),
                cache_v[l, :, gk, :].rearrange("(j p) e -> p j e", p=128), [], [cvt])
            cp(v_all[:, 0:256], cv[:, :256], [cvt], [t_v])
            for qg in range(NSQ):
                headnorm_cols(C_AT_K + gk * 128, 512 + qg * 512, 512, kcol, qg * 512, kT_all, t_kT, 256 + qg * 512)
            to_tokmajor_bf(C_AT_V + gk * 128, 512, TS // 128, v_all, t_v, 2)
            for hh in range(2):
                h = gk * 2 + hh
                for qg in range(NSQ):
                    headnorm_cols(C_AT_Q + h * 128, 512 + qg * 512, 512, qcol, qg * 512, qT_bf, t_q, qg * 512)
                for qg in range(NSQ):
                    attn_core(h, 512, qg * 512, 2 + TS // 128, 512 + qg * 512)


    rf = Ring([arena[:, i * 1024:(i + 1) * 1024].bitcast(F32) for i in range(22)])
    rb = Ring([arena[:, 22528 + i * 512:22528 + (i + 1) * 512] for i in range(16)])
    L4 = [sb("L4_%d" % d, [128, 512]) for d in range(2)]
    for d in range(2):
        for j in range(4):
            dma(L4[d][:, j * 128:(j + 1) * 128], tri_in[d], [], [t_const])
    normw_sb = sb("normw_sb", [128, L, 3])
    dma(normw_sb[:], normw, [], [t_const])
    ones_f = sb("ones_f", [128, 128])
    memset(ones_f[:], 1.0, [t_const])
    w2a = sb("w2a", [33, 2, 512])
    t_w2a = Tok("w2a")
    t_o = {}


    class _Chain:
        pass

    CH = []
    for ci_ in range(2):
        R = _Chain()
        R.psum = Ring(_banks[ci_ * 4:(ci_ + 1) * 4], (psum.toks + pacc.toks)[ci_ * 4:(ci_ + 1) * 4])
        R.lra = Ring([sb("lra%d_%d" % (ci_, i), [33, 128]) for i in range(2)])
        for b_ in R.lra.bufs:
            memset(b_[:], 0.0, [t_const])
            memset(b_[32:33, :], 1.0, [t_const])
        R.Sst = sb("Sst%d" % ci_, [128, 4, 132])
        R.Sbf = sb("Sbf%d" % ci_, [128, 4, 132], BF16)
        R.t_S, R.t_Sbf = Tok("S%d" % ci_), Tok("Sbf%d" % ci_)
        R.gsm = Ring([sb("gsm%d_%d" % (ci_, i), [128, 128]) for i in range(4)])
        R.vaug = Ring([sb("vaug%d_%d" % (ci_, i), [128, 4, 132], BF16) for i in range(2)])
        for b_ in R.vaug.bufs:
            memset(b_[:], 1.0, [t_const])
        R.nbf = sb("nbf%d" % ci_, [128, 4, 128], BF16)
        R.t_nbf = Tok("nbf%d" % ci_)
        R.mrow = sb("mrow%d" % ci_, [4, 2])
        R.t_mrow = Tok("mrow%d" % ci_)
        CH.append(R)

    def run_chains(gens):
        live = list(gens)
        while live:
            nxt = []
            for g_ in live:
                try:
                    next(g_)
                    nxt.append(g_)
                except StopIteration:
                    pass
            live = nxt

    def seqs():
        return [("p", 0, 0, 256), ("p", 1, 256, 256), ("s", 0, 512, TS)]

    def load4(row0, c0, n=128):
        t, tt_ = rf.next()
        dma(t[:, :4 * n].rearrange("p (h t) -> p h t", h=4),
            zT[row0:row0 + 512, c0:c0 + n].rearrange("(h p) t -> p h t", p=128),
            ztoks(row0, row0 + 512, c0, c0 + n), [tt_])
        return t, tt_

    def tr4(src, srct, dst_bf, dstt, ring=None):
        ps, pt = (ring or psum).next()
        for h in range(4):
            tr(ps[:, h * 128:(h + 1) * 128], src[:, h * 128:(h + 1) * 128], [srct], [pt])
        cp(dst_bf[:, :512], ps[:], [pt], [dstt], eng="act")

    def gla_chain(l, kind, sq, col0, T, d, R):
        if True:
            ncn = T // 128
            if True:
                psum, lra, Sst, Sbf, t_S, t_Sbf = R.psum, R.lra, R.Sst, R.Sbf, R.t_S, R.t_Sbf
                end = 127 if d == 0 else 0
                if kind == "p":
                    memset(Sst[:], 0.0, [t_S])
                    memset(Sbf[:], 0.0, [t_Sbf])
                else:
                    dma(Sst[:, :, 0:128], state_gla[l, d].rearrange("h k e -> k h e"), [], [t_S])
                    yield
                    cp(Sbf[:, :, 0:128], Sst[:, :, 0:128], [t_S], [t_Sbf], eng="act")
                    yield
                for ci in (range(ncn) if d == 0 else range(ncn - 1, -1, -1)):
                    c0 = col0 + ci * 128
                    lr_, lrt = lra.next()
                    dma(lr_[0:16, :], zT[C_GLA_LR + d * 16:C_GLA_LR + d * 16 + 16, c0:c0 + 128],
                        ztoks(C_GLA_LR, C_GLA_LR + 32, c0, c0 + 128), [lrt])
                    pp, ppt = psum.next()
                    mm(pp[:], lr_[0:33, :], w2a[0:33, d, :], True, True, [lrt, t_w2a, t_const], [ppt])
                    e1, e1t = rf.next()
                    act(e1[:], pp[:], AF.Exp, [ppt], [e1t], scale=-1.0)
                    yield
                    sp_, spt = rf.next()
                    act(sp_[:], e1[:], AF.Ln, [e1t], [spt], bias=one_col[:], scale=1.0)
                    yield
                    pB, pBt = psum.next()
                    for h in range(4):
                        mm(pB[:, h * 128:(h + 1) * 128], sp_[:, h * 128:(h + 1) * 128], L4[d][:, 0:128], True, True,
                           [spt, t_const], [pBt])
                    epos, epost = rf.next()
                    act(epos[:], pB[:], AF.Exp, [pBt], [epost], scale=-1.0 / 16)
                    yield
                    eneg, enegt = rf.next()
                    act(eneg[:], pB[:], AF.Exp, [pBt], [enegt], scale=1.0 / 16)
                    yield
                    q4, q4t = load4(C_GLA_Q, c0)
                    k4, k4t = load4(C_GLA_K, c0)
                    v4, v4t = load4(C_GLA_V, c0)
                    qs, qst = rb.next()
                    stt(qs[:], q4[:], HD ** -0.5, epos[:], ALU.mult, ALU.mult, [q4t, epost], [qst])
                    yield
                    tt(k4[:], k4[:], eneg[:], ALU.mult, [k4t, enegt], [k4t])
                    yield
                    ks, kst = rb.next()
                    cp(ks[:], k4[:], [k4t], [kst], eng="act")
                    yield
                    kh, kht = rf.next()
                    for h in range(4):
                        ts(kh[:, h * 128:(h + 1) * 128], k4[:, h * 128:(h + 1) * 128],
                           epos[:, h * 128 + end:h * 128 + end + 1], None, ALU.mult, None, [k4t, epost], [kht])
                    khT, khTt = rb.next()
                    tr4(kh, kht, khT, khTt, psum)
                    yield
                    vT, vTt = rb.next()
                    tr4(v4, v4t, vT, vTt, psum)
                    yield
                    pA, pAt = psum.next()
                    for h in range(4):
                        mm(pA[:, h * 128:(h + 1) * 128], ks[:, h * 128:(h + 1) * 128], qs[:, h * 128:(h + 1) * 128],
                           True, True, [kst, qst], [pAt])
                    am, amt = rb.next()
                    tt(am[:], pA[:], L4[d][:], ALU.mult, [pAt, t_const], [amt])
                    yield
                    pO, pOt = psum.next()
                    for h in range(4):
                        mm(pO[:, h * 128:(h + 1) * 128], vT[:, h * 128:(h + 1) * 128], am[:, h * 128:(h + 1) * 128],
                           True, False, [vTt, amt], [pOt])
                        mm(pO[:, h * 128:(h + 1) * 128], Sbf[:, h, 0:128], qs[:, h * 128:(h + 1) * 128],
                           False, True, [t_Sbf, qst], [pOt])
                    ob, obt = rf.next()
                    cp(ob[:], pO[:], [pOt], [obt], eng="act")
                    yield
                    dma(oT[d, 0:512, c0:c0 + 128].rearrange("(h p) t -> p h t", p=128),
                        ob[:].rearrange("p (h t) -> p h t", h=4), [obt], [tk(t_o, (d, 0, c0 // 512))])
                    pU, pUt = psum.next()
                    for h in range(4):
                        mm(pU[:, h * 128:(h + 1) * 128], khT[:, h * 128:(h + 1) * 128], vT[:, h * 128:(h + 1) * 128],
                           True, True, [khTt, vTt], [pUt])
                    for h in range(4):
                        stt(Sst[:, h, 0:128], Sst[:, h, 0:128], epos[:, h * 128 + end:h * 128 + end + 1],
                            pU[:, h * 128:(h + 1) * 128], ALU.mult, ALU.add, [t_S, epost, pUt], [t_S])
                    cp(Sbf[:, :, 0:128], Sst[:, :, 0:128], [t_S], [t_Sbf], eng="act")
                    yield
                if kind == "p":
                    dma(nsg[sq, l, d].rearrange("h k e -> k h e"), Sst[:, :, 0:128], [t_S], [t_out])
                    yield


    def gla(l):
        dma(w2a[:], gla_w2aug[l].rearrange("d r c -> r d c"), [], [t_w2a])
        for (kind, sq, col0, T) in seqs():
            run_chains([gla_chain(l, kind, sq, col0, T, d, CH[d]) for d in range(2)])

    def post(l, mix, grow, gfunc):
        for h in range(4):
            for g in range(NTG):
                r0 = mix * 512 + h * 128
                a, at = rf.next()
                b2_, bt = rf.next()
                dma(a[:], oT[0, r0:r0 + 128, g * 512:(g + 1) * 512], [tk(t_o, (0, mix, g))], [at])
                dma(b2_[:], oT[1, r0:r0 + 128, g * 512:(g + 1) * 512], [tk(t_o, (1, mix, g))], [bt])
                tt(a[:], a[:], b2_[:], ALU.add, [at, bt], [at])
                sq_, sqt = rb.next()
                act(sq_[:], a[:], AF.Square, [at], [sqt])
                ps, pt = psum.next()
                mm(ps[:], ones_bf[:], sq_[:], True, True, [sqt, t_const], [pt])
                rs, rst = rf.next()
                act(rs[:], ps[:], AF.Sqrt, [pt], [rst], bias=eps_col[:], scale=1.0 / HD)
                recip(rs[:], rs[:], [rst], [rst])
                stt(a[:], a[:], normw_sb[:, l, mix:mix + 1], rs[:], ALU.mult, ALU.mult, [at, rst, t_const], [at])
                zg, zgt = rf.next()
                gr = grow + h * 128
                dma(zg[:], zT[gr:gr + 128, g * 512:(g + 1) * 512], ztoks(gr, gr + 128, g * 512, (g + 1) * 512), [zgt])
                act(zg[:], zg[:], gfunc, [zgt], [zgt])
                yb, ybt = rb.next()
                tt(yb[:], a[:], zg[:], ALU.mult, [at, zgt], [ybt])
                dma(ybrT[r0:r0 + 128, g * 512:(g + 1) * 512], yb[:], [ybt], [tk(t_ybr, (r0 // 128, g))])


    gsm = Ring([sb("gsm%d" % i, [128, 128]) for i in range(4)])
    mlc = sb("mlc", [128, L, 2, 12])
    dma(mlc[:, :, :, 0:8], ml_gb, [], [t_const])
    dma(mlc[:, :, :, 8:12], ml_m0b, [], [t_const])

    def mlstm_chain(l, kind, sq, col0, T, d, R):
        if True:
            ncn = T // 128
            if True:
                psum, pacc, Sst, Sbf, t_S, t_Sbf = R.psum, R.psum, R.Sst, R.Sbf, R.t_S, R.t_Sbf
                gsm, vaug, nbf, t_nbf, mrow, t_mrow = R.gsm, R.vaug, R.nbf, R.t_nbf, R.mrow, R.t_mrow
                if kind == "p":
                    memset(Sst[:], 0.0, [t_S])
                    memset(mrow[:], 0.0, [t_mrow])
                else:
                    dma(Sst[:, :, 0:128], state_mc[l, d].rearrange("h k e -> k h e"), [], [t_S])
                    yield
                    n0, n0t = gsm.next()
                    dma(n0[:, 0:4], ml_n0T[l, d], [], [n0t])
                    yield
                    act(n0[:, 4:8], mlc[:, l, d, 8:12], AF.Exp, [t_const, n0t], [n0t])
                    yield
                    cp(Sst[:, :, 128:129], n0[:, 0:4].rearrange("p (h o) -> p h o", o=1), [n0t, t_S], [t_S])
                    yield
                    for h in range(4):
                        ts(Sst[:, h, 0:129], Sst[:, h, 0:129], n0[:, 4 + h:5 + h], None, ALU.mult, None, [t_S, n0t], [t_S])
                cp(Sbf[:, :, 0:128], Sst[:, :, 0:128], [t_S], [t_Sbf], eng="act")
                for h in range(4):
                    ts(nbf[:, h, :], ones_f[:], Sst[:, h, 128:129], None, ALU.mult, None, [t_S, t_const], [t_nbf])
                    yield
                for ci in (range(ncn) if d == 0 else range(ncn - 1, -1, -1)):
                    c0 = col0 + ci * 128
                    gT, gTt = gsm.next()
                    r0 = C_ML_IF + d * 8
                    dma(gT[0:8, 0:128], zT[r0:r0 + 8, c0:c0 + 128], ztoks(r0, r0 + 8, c0, c0 + 128), [gTt])
                    yield
                    pg, pgt = psum.next()
                    tr(pg[:, 0:8], gT[0:8, 0:128], [gTt], [pgt])
                    G, Gt = gsm.next()
                    tt(G[:, 0:8], pg[:, 0:8], mlc[:, l, d, 0:8], ALU.add, [pgt, t_const], [Gt])
                    yield
                    act(G[:, 8:12], G[:, 4:8], AF.Exp, [Gt], [Gt], scale=-1.0)
                    yield
                    act(G[:, 8:12], G[:, 8:12], AF.Ln, [Gt], [Gt], bias=one_col[:], scale=1.0)
                    yield
                    pF, pFt = psum.next()
                    mm(pF[:, 0:4], L4[d][:, 0:128], G[:, 8:12], True, True, [Gt, t_const], [pFt])
                    mm(pF[:, 4:8], ones_f[:], G[:, 8:12], True, True, [Gt, t_const], [pFt])
                    cp(G[:, 32:40], pF[:, 0:8], [pFt, Gt], [Gt])
                    yield
                    tt(G[:, 12:16], G[:, 0:4], G[:, 32:36], ALU.add, [Gt], [Gt])
                    yield
                    act(G[:, 16:20], G[:, 12:16], AF.Exp, [Gt], [Gt])
                    yield
                    act(G[:, 20:24], G[:, 32:36], AF.Exp, [Gt], [Gt])
                    yield
                    act(G[:, 24:28], G[:, 36:40], AF.Exp, [Gt], [Gt], scale=-1.0)
                    yield
                    tt(G[:, 28:32], G[:, 16:20], G[:, 24:28], ALU.mult, [Gt], [Gt])
                    yield
                    if kind == "p":
                        pm, pmt = psum.next()
                        tr(pm[0:4, 0:128], G[:, 12:16], [Gt], [pmt])
                        mx, mxt = gsm.next()
                        P.op("dve", (lambda mx=mx, pm=pm: lambda e: e.reduce_max(out=mx[0:4, 0:1], in_=pm[0:4, 0:128],
                                                                             axis=mybir.AxisListType.X))(), [pmt], [mxt])
                        pm2, pm2t = psum.next()
                        tr(pm2[0:4, 0:128], G[:, 36:40], [Gt], [pm2t])
                        tt(mrow[0:4, 0:1], mrow[0:4, 0:1], mx[0:4, 0:1], ALU.max, [t_mrow, mxt], [t_mrow])
                        tt(mrow[0:4, 0:1], mrow[0:4, 0:1], pm2[0:4, 0:1], ALU.subtract, [t_mrow, pm2t], [t_mrow])
                    q4, q4t = load4(C_ML_Q, c0)
                    k4, k4t = load4(C_ML_K, c0)
                    v4, v4t = load4(C_ML_V, c0)
                    ts(k4[:], k4[:], HD ** -0.5, None, ALU.mult, None, [k4t], [k4t])
                    yield
                    qb, qbt = rb.next()
                    cp(qb[:], q4[:], [q4t], [qbt], eng="act")
                    yield
                    kb, kbt = rb.next()
                    cp(kb[:], k4[:], [k4t], [kbt], eng="act")
                    yield
                    pA, pAt = psum.next()
                    for h in range(4):
                        mm(pA[:, h * 128:(h + 1) * 128], kb[:, h * 128:(h + 1) * 128], qb[:, h * 128:(h + 1) * 128],
                           True, True, [kbt, qbt], [pAt])
                    am, amt = rb.next()
                    for h in range(4):
                        stt(am[:, h * 128:(h + 1) * 128], pA[:, h * 128:(h + 1) * 128], G[:, 16 + h:17 + h], L4[d][:, 0:128],
                            ALU.mult, ALU.mult, [pAt, Gt, t_const], [amt])
                    pK, pKt = psum.next()
                    for h in range(4):
                        tr(pK[:, h * 128:(h + 1) * 128], k4[:, h * 128:(h + 1) * 128], [k4t], [pKt])
                    kc, kct = rb.next()
                    for h in range(4):
                        ts(kc[:, h * 128:(h + 1) * 128], pK[:, h * 128:(h + 1) * 128], G[:, 28 + h:29 + h], None, ALU.mult, None,
                           [pKt, Gt], [kct])
                    pV, pVt = psum.next()
                    for h in range(4):
                        tr(pV[:, h * 128:(h + 1) * 128], v4[:, h * 128:(h + 1) * 128], [v4t], [pVt])
                    va, vat = vaug.next()
                    cp(va[:, :, 0:128], pV[:].rearrange("p (h e) -> p h e", h=4), [pVt], [vat], eng="act")
                    yield
                    pO, pOt = psum.next()
                    pD, pDt = pacc.next()
                    pE, pEt = pacc.next()
                    dg, dgt = rf.next()
                    for h in range(4):
                        mm(pO[:, h * 128:(h + 1) * 128], va[:, h, 0:128], am[:, h * 128:(h + 1) * 128], True, False, [vat, amt], [pOt])
                        mm(pO[:, h * 128:(h + 1) * 128], Sbf[:, h, 0:128], qb[:, h * 128:(h + 1) * 128], False, True, [t_Sbf, qbt], [pOt])
                        mm(pD[:, h * 128:(h + 1) * 128], ones_bf[:], am[:, h * 128:(h + 1) * 128], True, False, [t_const, amt], [pDt])
                        mm(pD[:, h * 128:(h + 1) * 128], nbf[:, h, :], qb[:, h * 128:(h + 1) * 128], False, True, [t_nbf, qbt], [pDt])
                        ts(dg[:, h * 128:(h + 1) * 128], ident[:], G[:, 20 + h:21 + h], None, ALU.mult, None, [Gt, t_const], [dgt])
                        mm(pE[:, h * 128:(h + 1) * 128], ones_f[:], dg[:, h * 128:(h + 1) * 128], True, True, [t_const, dgt], [pEt])
                    bcs, bcst = rf.next()
                    cp(bcs[:], pE[:], [pEt], [bcst], eng="act")
                    yield
                    dab, dabt = rf.next()
                    act(dab[:], pD[:], AF.Abs, [pDt], [dabt])
                    yield
                    tt(dab[:], dab[:], bcs[:], ALU.max, [dabt, bcst], [dabt])
                    yield
                    recip(dab[:], dab[:], [dabt], [dabt])
                    yield
                    ob, obt = rf.next()
                    tt(ob[:], pO[:], dab[:], ALU.mult, [pOt, dabt], [obt])
                    yield
                    dma(oT[d, 512:1024, c0:c0 + 128].rearrange("(h p) t -> p h t", p=128),
                        ob[:].rearrange("p (h t) -> p h t", h=4), [obt], [tk(t_o, (d, 1, c0 // 512))])
                    pU1, pU1t = psum.next()
                    pU2, pU2t = psum.next()
                    for h in range(4):
                        pu, put = (pU1, pU1t) if h < 2 else (pU2, pU2t)
                        o_ = (h % 2) * 132
                        mm(pu[:, o_:o_ + 129], kc[:, h * 128:(h + 1) * 128], va[:, h, 0:129], True, True, [kct, vat], [put])
                    for h in range(4):
                        pu, put = (pU1, pU1t) if h < 2 else (pU2, pU2t)
                        o_ = (h % 2) * 132
                        stt(Sst[:, h, 0:129], Sst[:, h, 0:129], G[:, 24 + h:25 + h], pu[:, o_:o_ + 129], ALU.mult, ALU.add,
                            [t_S, Gt, put], [t_S])
                    cp(Sbf[:, :, 0:128], Sst[:, :, 0:128], [t_S], [t_Sbf], eng="act")
                    yield
                    for h in range(4):
                        ts(nbf[:, h, :], ones_f[:], Sst[:, h, 128:129], None, ALU.mult, None, [t_S, t_const], [t_nbf])
                if kind == "p":
                    dgm, dgmt = gsm.next()
                    ts(dgm[0:4, 0:4], ident[0:4, 0:4], mrow[0:4, 0:1], None, ALU.mult, None, [t_mrow, t_const], [dgmt])
                    yield
                    pmb, pmbt = psum.next()
                    mm(pmb[:, 0:4], ones_f[0:4, :], dgm[0:4, 0:4], True, True, [dgmt, t_const], [pmbt])
                    act(dgm[:, 8:12], pmb[:, 0:4], AF.Exp, [pmbt, dgmt], [dgmt], scale=-1.0)
                    yield
                    so, sot = rf.next()
                    for h in range(4):
                        ts(so[:, h * 128:(h + 1) * 128], Sst[:, h, 0:128], dgm[:, 8 + h:9 + h], None, ALU.mult, None, [t_S, dgmt], [sot])
                    tt(dgm[:, 12:16], Sst[:, :, 128:129].rearrange("p h o -> p (h o)"), dgm[:, 8:12], ALU.mult, [t_S, dgmt], [dgmt])
                    yield
                    dma(nsc[sq, l, d].rearrange("h k e -> k h e"), so[:].rearrange("p (h e) -> p h e", h=4), [sot], [t_out])
                    yield
                    pn, pnt = psum.next()
                    tr(pn[0:4, 0:128], dgm[:, 12:16], [dgmt], [pnt])
                    nso, nsot = gsm.next()
                    cp(nso[0:4, 0:128], pn[0:4, 0:128], [pnt], [nsot])
                    yield
                    dma(nsn[sq, l, d], nso[0:4, 0:128], [nsot], [t_out])
                    yield
                    dma(nsm[sq, l, d:d + 1, :].rearrange("o h -> h o"), mrow[0:4, 0:1], [t_mrow], [t_out])
                    yield


    def mlstm(l):
        for (kind, sq, col0, T) in seqs():
            run_chains([mlstm_chain(l, kind, sq, col0, T, d, CH[d]) for d in range(2)])

    GT = [arena[:, i * 1024:(i + 1) * 1024].bitcast(F32) for i in range(33)]
    GTt = [Tok("gt%d" % i) for i in range(33)]
    cx = arena[:, 22528:22528 + 4112].bitcast(F32)
    cacc = arena[:, 26640:26640 + 4096].bitcast(F32)
    t_cx, t_cacc = Tok("cx"), Tok("cacc")
    S4 = [sb("S4_%d" % d, [128, 512]) for d in range(2)]
    I4 = sb("I4", [128, 512])
    for j in range(4):
        for d in range(2):
            dma(S4[d][:, j * 128:(j + 1) * 128], strm_in[d], [], [t_const])
        dma(I4[:, j * 128:(j + 1) * 128], ident_in, [], [t_const])
    cw = sb("cw", [128, 12, 5])
    t_cw = Tok("cw")
    gdc = sb("gdc", [128, L, 2, 8])
    dma(gdc[:], gd_gb, [], [t_const])
    t_gq = {}

    def gdn_conv(l):
        dma(cw[:], gd_cwT[l], [], [t_cw])
        for (kind, sq, col0, T) in seqs():
            for blk in range(12):
                part = blk // 4
                r0 = C_GD_QKV + blk * 128
                memset(cx[:, 0:2], 0.0, [t_cx])
                memset(cx[:, T + 2:T + 4], 0.0, [t_cx])
                dma(cx[:, 2:T + 2], zT[r0:r0 + 128, col0:col0 + T], ztoks(r0, r0 + 128, col0, col0 + T), [t_cx])
                ts(cacc[:, 0:T], cx[:, 0:T], cw[:, blk, 0:1], None, ALU.mult, None, [t_cx, t_cw], [t_cacc])
                for j in range(1, 5):
                    stt(cacc[:, 0:T], cx[:, j:j + T], cw[:, blk, j:j + 1], cacc[:, 0:T], ALU.mult, ALU.add,
                        [t_cx, t_cw, t_cacc], [t_cacc])
                act(cacc[:, 0:T], cacc[:, 0:T], AF.Silu, [t_cacc], [t_cacc])
                for pc in range(0, T, 512):
                    n = min(512, T - pc)
                    if part < 2:
                        sq_, sqt = tmpb.next()
                        act(sq_[:, :n], cacc[:, pc:pc + n], AF.Square, [t_cacc], [sqt])
                        ps, pt = psum.next()
                        mm(ps[:, :n], ones_bf[:], sq_[:, :n], True, True, [sqt, t_const], [pt])
                        rs, rst = tmpf.next()
                        act(rs[:, :n], ps[:, :n], AF.Sqrt, [pt], [rst], bias=eps_col[:], scale=1.0)
                        recip(rs[:, :n], rs[:, :n], [rst], [rst])
                        if part == 0:
                            stt(cacc[:, pc:pc + n], cacc[:, pc:pc + n], HD ** -0.5, rs[:, :n], ALU.mult, ALU.mult,
                                [t_cacc, rst], [t_cacc])
                        else:
                            tt(cacc[:, pc:pc + n], cacc[:, pc:pc + n], rs[:, :n], ALU.mult, [t_cacc, rst], [t_cacc])
                    ps2, pt2 = psum.next()
                    nb_ = n // 128
                    for j in range(nb_):
                        tr(ps2[:, j * 128:(j + 1) * 128], cacc[:, pc + j * 128:pc + (j + 1) * 128], [t_cacc], [pt2])
                    o, ot = tmpf.next()
                    cp(o[:, :n], ps2[:, :n], [pt2], [ot], eng="act")
                    tok0 = col0 + pc
                    dma(gqkv[tok0:tok0 + n, blk * 128:(blk + 1) * 128].rearrange("(j t) c -> t j c", t=128),
                        o[:, :n].rearrange("t (j c) -> t j c", j=nb_), [ot], [tk(t_gq, (tok0 // 512, blk))])

    def gdn(l):
        (qtok, ktok, vtok, kp, kb, qp, rv, kh, kT, kbT, qT, qpT, X, XT, X2, XT2, RT, AT, solv, solk, PT, McT, Sg, obg, Dm, DTs, DTi, tmpD) = range(28)

        def g4(i, h):
            return GT[i][:, h * 128:(h + 1) * 128]

        for (kind, sq, col0, T) in seqs():
            ncn = T // 128
            for d in range(2):
                if kind == "p":
                    memset(GT[Sg][:], 0.0, [GTt[Sg]])
                else:
                    dma(GT[Sg][:].rearrange("k (h e) -> k h e", h=4), state_gd[l, d].rearrange("h k e -> k h e"), [], [GTt[Sg]])
                for ci in (range(ncn) if d == 0 else range(ncn - 1, -1, -1)):
                    c0 = col0 + ci * 128
                    gT_, gTt_ = gsm.next()
                    r0 = C_GD_AB + d * 8
                    dma(gT_[0:8, 0:128], zT[r0:r0 + 8, c0:c0 + 128], ztoks(r0, r0 + 8, c0, c0 + 128), [gTt_])
                    pg, pgt = psum.next()
                    tr(pg[:, 0:8], gT_[0:8, 0:128], [gTt_], [pgt])
                    G, Gt = gsm.next()
                    cp(G[:, 0:8], pg[:, 0:8], [pgt], [Gt])
                    tt(G[:, 8:12], G[:, 0:4], gdc[:, l, d, 0:4], ALU.add, [Gt, t_const], [Gt])
                    act(G[:, 8:12], G[:, 8:12], AF.Exp, [Gt], [Gt])
                    act(G[:, 8:12], G[:, 8:12], AF.Ln, [Gt], [Gt], bias=one_col[:], scale=1.0)
                    act(G[:, 12:16], gdc[:, l, d, 4:8], AF.Exp, [Gt, t_const], [Gt])
                    tt(G[:, 8:12], G[:, 8:12], G[:, 12:16], ALU.mult, [Gt], [Gt])
                    act(G[:, 16:20], G[:, 4:8], AF.Exp, [Gt], [Gt], scale=-1.0)
                    ts(G[:, 16:20], G[:, 16:20], 1.0, None, ALU.add, None, [Gt], [Gt])
                    recip(G[:, 16:20], G[:, 16:20], [Gt], [Gt])
                    pF, pFt = psum.next()
                    mm(pF[:, 0:4], L4[d][:, 0:128], G[:, 8:12], True, True, [Gt, t_const], [pFt])
                    mm(pF[:, 4:8], ones_f[:], G[:, 8:12], True, True, [Gt, t_const], [pFt])
                    cp(G[:, 20:28], pF[:, 0:8], [pFt, Gt], [Gt])
                    act(G[:, 28:32], G[:, 20:24], AF.Exp, [Gt], [Gt], scale=-1.0)
                    act(G[:, 36:40], G[:, 24:28], AF.Exp, [Gt], [Gt], scale=-1.0)
                    tt(G[:, 40:44], G[:, 16:20], G[:, 28:32], ALU.mult, [Gt], [Gt])
                    tt(G[:, 44:48], G[:, 24:28], G[:, 20:24], ALU.subtract, [Gt], [Gt])
                    act(G[:, 44:48], G[:, 44:48], AF.Exp, [Gt], [Gt], scale=-1.0)
                    pGb, pGbt = psum.next()
                    for h in range(4):
                        ts(g4(tmpD, h), ident[:], G[:, 20 + h:21 + h], None, ALU.mult, None, [Gt, t_const], [GTt[tmpD]])
                        mm(pGb[:, h * 128:(h + 1) * 128], ones_f[:], g4(tmpD, h), True, True, [GTt[tmpD], t_const], [pGbt])
                    for h in range(4):
                        ts(g4(Dm, h), pGb[:, h * 128:(h + 1) * 128], G[:, 20 + h:21 + h], 0.0, ALU.subtract, ALU.min, [pGbt, Gt], [GTt[Dm]])
                        ts(g4(DTs, h), pGb[:, h * 128:(h + 1) * 128], G[:, 20 + h:21 + h], 0.0, ALU.subtract, ALU.max, [pGbt, Gt], [GTt[DTs]])
                    act(GT[Dm][:], GT[Dm][:], AF.Exp, [GTt[Dm]], [GTt[Dm]])
                    act(GT[DTs][:], GT[DTs][:], AF.Exp, [GTt[DTs]], [GTt[DTs]], scale=-1.0)
                    tt(GT[Dm][:], GT[Dm][:], S4[1 - d][:], ALU.mult, [GTt[Dm], t_const], [GTt[Dm]])
                    tt(GT[DTi][:], GT[DTs][:], L4[d][:], ALU.mult, [GTt[DTs], t_const], [GTt[DTi]])
                    tt(GT[DTs][:], GT[DTs][:], S4[d][:], ALU.mult, [GTt[DTs], t_const], [GTt[DTs]])
                    for i, part in ((qtok, 0), (ktok, 1), (vtok, 2)):
                        dma(GT[i][:], gqkv[c0:c0 + 128, part * 512:(part + 1) * 512],
                            [tk(t_gq, (c0 // 512, part * 4 + h)) for h in range(4)], [GTt[i]])
                    for h in range(4):
                        ts(g4(kp, h), g4(ktok, h), G[:, 40 + h:41 + h], None, ALU.mult, None, [GTt[ktok], Gt], [GTt[kp]])
                        ts(g4(kb, h), g4(ktok, h), G[:, 16 + h:17 + h], None, ALU.mult, None, [GTt[ktok], Gt], [GTt[kb]])
                        ts(g4(qp, h), g4(qtok, h), G[:, 28 + h:29 + h], None, ALU.mult, None, [GTt[qtok], Gt], [GTt[qp]])
                        ts(g4(rv, h), g4(vtok, h), G[:, 16 + h:17 + h], None, ALU.mult, None, [GTt[vtok], Gt], [GTt[rv]])
                        ts(g4(kh, h), g4(ktok, h), G[:, 44 + h:45 + h], None, ALU.mult, None, [GTt[ktok], Gt], [GTt[kh]])
                    for src, dst in ((ktok, kT), (kb, kbT), (qtok, qT), (qp, qpT)):
                        ps, pt = psum.next()
                        for h in range(4):
                            tr(ps[:, h * 128:(h + 1) * 128], g4(src, h), [GTt[src]], [pt])
                        cp(GT[dst][:], ps[:], [pt], [GTt[dst]], eng="act")
                    ps, pt = psum.next()
                    for h in range(4):
                        mm(ps[:, h * 128:(h + 1) * 128], g4(kbT, h), g4(kT, h), True, True, [GTt[kbT], GTt[kT]], [pt])
                    tt(GT[X][:], ps[:], GT[Dm][:], ALU.mult, [pt, GTt[Dm]], [GTt[X]])
                    ps, pt = psum.next()
                    for h in range(4):
                        mm(ps[:, h * 128:(h + 1) * 128], g4(kT, h), g4(kbT, h), True, True, [GTt[kbT], GTt[kT]], [pt])
                    tt(GT[XT][:], ps[:], GT[DTs][:], ALU.mult, [pt, GTt[DTs]], [GTt[XT]])
                    tt(GT[RT][:], I4[:], GT[XT][:], ALU.subtract, [t_const, GTt[XT]], [GTt[RT]])
                    xa, xta, xb_, xtb = X, XT, X2, XT2
                    for step in range(6):
                        ps, pt = psum.next()
                        for h in range(4):
                            mm(ps[:, h * 128:(h + 1) * 128], g4(xta, h), g4(xa, h), True, True, [GTt[xa], GTt[xta]], [pt])
                        cp(GT[xb_][:], ps[:], [pt], [GTt[xb_]], eng="act")
                        if step < 5:
                            ps2, pt2 = psum.next()
                            for h in range(4):
                                mm(ps2[:, h * 128:(h + 1) * 128], g4(xa, h), g4(xta, h), True, True, [GTt[xa], GTt[xta]], [pt2])
                            cp(GT[xtb][:], ps2[:], [pt2], [GTt[xtb]])
                        ps3, pt3 = psum.next()
                        for h in range(4):
                            mm(ps3[:, h * 128:(h + 1) * 128], g4(xb_, h), g4(RT, h), True, True, [GTt[xb_], GTt[RT]], [pt3])
                        tt(GT[RT][:], GT[RT][:], ps3[:], ALU.add, [GTt[RT], pt3], [GTt[RT]])
                        xa, xta, xb_, xtb = xb_, xtb, xa, xta
                    ps, pt = psum.next()
                    for h in range(4):
                        mm(ps[:, h * 128:(h + 1) * 128], g4(kT, h), g4(qT, h), True, True, [GTt[kT], GTt[qT]], [pt])
                    tt(GT[AT][:], ps[:], GT[DTi][:], ALU.mult, [pt, GTt[DTi]], [GTt[AT]])
                    ps, pt = psum.next()
                    for h in range(4):
                        mm(ps[:, h * 128:(h + 1) * 128], g4(RT, h), g4(rv, h), True, True, [GTt[RT], GTt[rv]], [pt])
                    cp(GT[solv][:], ps[:], [pt], [GTt[solv]], eng="act")
                    ps, pt = psum.next()
                    for h in range(4):
                        mm(ps[:, h * 128:(h + 1) * 128], g4(RT, h), g4(kp, h), True, True, [GTt[RT], GTt[kp]], [pt])
                    cp(GT[solk][:], ps[:], [pt], [GTt[solk]])
                    ps, pt = psum.next()
                    for h in range(4):
                        mm(ps[:, h * 128:(h + 1) * 128], g4(solk, h), g4(AT, h), True, True, [GTt[solk], GTt[AT]], [pt])
                    tt(GT[PT][:], GT[qpT][:], ps[:], ALU.subtract, [GTt[qpT], pt], [GTt[PT]])
                    ps, pt = psum.next()
                    for h in range(4):
                        mm(ps[:, h * 128:(h + 1) * 128], g4(solv, h), g4(AT, h), True, False, [GTt[solv], GTt[AT]], [pt])
                        mm(ps[:, h * 128:(h + 1) * 128], g4(Sg, h), g4(PT, h), False, True, [GTt[Sg], GTt[PT]], [pt])
                    cp(GT[obg][:], ps[:], [pt], [GTt[obg]], eng="act")
                    dma(oT[d, 1024:1536, c0:c0 + 128].rearrange("(h p) t -> p h t", p=128),
                        GT[obg][:].rearrange("p (h t) -> p h t", h=4), [GTt[obg]], [tk(t_o, (d, 2, c0 // 512))])
                    ps, pt = psum.next()
                    for h in range(4):
                        mm(ps[:, h * 128:(h + 1) * 128], g4(solk, h), g4(kh, h), True, True, [GTt[solk], GTt[kh]], [pt])
                    for h in range(4):
                        stt(g4(McT, h), ident[:], G[:, 36 + h:37 + h], ps[:, h * 128:(h + 1) * 128], ALU.mult, ALU.subtract,
                            [pt, Gt, t_const], [GTt[McT]])
                    ps, pt = psum.next()
                    for h in range(4):
                        mm(ps[:, h * 128:(h + 1) * 128], g4(kh, h), g4(solv, h), True, False, [GTt[kh], GTt[solv]], [pt])
                        mm(ps[:, h * 128:(h + 1) * 128], g4(McT, h), g4(Sg, h), False, True, [GTt[McT], GTt[Sg]], [pt])
                    cp(GT[Sg][:], ps[:], [pt], [GTt[Sg]])
                if kind == "p":
                    dma(nsd[sq, l, d].rearrange("h k e -> k h e"), GT[Sg][:].rearrange("k (h e) -> k h e", h=4),
                        [GTt[Sg]], [t_out])

    one_col = sb("one_col", [128, 1])
    memset(one_col[:], 1.0, [t_const])

    def recurrent_stub(l, mixes=(0, 1, 2)):
        for r in [m * 4 + h for m in mixes for h in range(4)]:
            for g in range(NTG):
                dma(ybrT[r * 128:(r + 1) * 128, g * 512:(g + 1) * 512], zeros_bf[:], [t_const], [tk(t_ybr, (r, g))])
        for s in range(2):
            for d in range(2):
                for h in range(4):
                    if 0 in mixes:
                        dma(nsg[s, l, d, h], zeros[:, 0:128], [t_const], [t_out])
                    if 1 in mixes:
                        dma(nsc[s, l, d, h], zeros[:, 0:128], [t_const], [t_out])
                    if 2 in mixes:
                        dma(nsd[s, l, d, h], zeros[:, 0:128], [t_const], [t_out])
                if 1 in mixes:
                    dma(nsn[s, l, d], zeros[0:4, 0:128], [t_const], [t_out])
            if 1 in mixes:
                dma(nsm[s, l], zeros[0:2, 0:4], [t_const], [t_out])

    ybr_sb = arena[:, 16384:16384 + 8192].rearrange("p (k c) -> p k c", k=KC)
    t_ybrsb = Tok("ybrsb")
    wbr = Ring([arena[:, 24576 + i * 2048:24576 + (i + 1) * 2048].rearrange("p (n k c) -> p n k c", n=4, k=4) for i in range(2)])
    macc = sb("macc", [128, 512])
    t_macc = Tok("macc")

    def merge_out(l):
        for g in range(NTG):
            dma(ybr_sb, ybrT[:, g * 512:(g + 1) * 512].rearrange("(j p) t -> p j t", p=128),
                [tk(t_ybr, (r, g)) for r in range(16)], [t_ybrsb])
            for m in range(KC):
                wb, wbt = wbr.next()
                P.dma("pool", (lambda wb=wb, m=m: lambda e: e.dma_start(
                    out=wb, in_=w_branch[l][:, :, m * 128:(m + 1) * 128].rearrange("n (kc p) c -> p n kc c", p=128)))(),
                    [], [wbt])
                for n in range(4):
                    ps, pt = psum.next()
                    for kc in range(4):
                        mm(ps[:], wb[:, n, kc, :], ybr_sb[:, n * 4 + kc, :], kc == 0, kc == 3, [wbt, t_ybrsb], [pt])
                    zb, zbt = xblk.next()
                    r0 = C_MERGE + n * D + m * 128
                    dma(zb[:], zT[r0:r0 + 128, g * 512:(g + 1) * 512], ztoks(r0, r0 + 128, g * 512, (g + 1) * 512), [zbt])
                    act(zb[:], zb[:], AF.Sigmoid, [zbt], [zbt])
                    if n == 0:
                        tt(macc[:], ps[:], zb[:], ALU.mult, [pt, zbt], [t_macc])
                    else:
                        tt(zb[:], ps[:], zb[:], ALU.mult, [pt, zbt], [zbt])
                        if n < 3:
                            tt(macc[:], macc[:], zb[:], ALU.add, [t_macc, zbt], [t_macc])
                        else:
                            tt(hT[:, m, g * 512:(g + 1) * 512], macc[:], zb[:], ALU.add, [t_macc, zbt], [hT_tok[g]])
        for blk in range(D // 256):
            wt, wtok = wring.next()
            P.dma("pool", (lambda wt=wt, blk=blk: lambda e: e.dma_start(
                out=wt, in_=w_out[l][:, blk * 256:(blk + 1) * 256].rearrange("(kc p) c -> p kc c", p=128)))(), [], [wtok])
            for g in range(NTG):
                for mb in range(2):
                    ps, pt = psum.next()
                    for kc in range(KC):
                        mm(ps[:], wt[:, kc, mb * 128:(mb + 1) * 128], hT[:, kc, g * 512:(g + 1) * 512],
                           kc == 0, kc == KC - 1, [wtok, hT_tok[g]], [pt])
                    resid_update(ps[:], pt, blk * 2 + mb, g, 5 * 16)

    def final():
        for g in range(NTG):
            pss, pst = psum.next()
            for kc in range(KC):
                xb, xbt = xblk.next()
                dma(xb[:], xT[kc * 128:(kc + 1) * 128, g * 512:(g + 1) * 512], [t_xT[g]], [xbt])
                sq, sqt = tmpb.next()
                act(sq[:], xb[:], AF.Square, [xbt], [sqt])
                mm(pss[:], ones_bf[:], sq[:], kc == 0, kc == KC - 1, [sqt, t_const], [pst])
            act(rstd[:], pss[:], AF.Sqrt, [pst], [t_rstd], bias=eps_col[:], scale=1.0 / D)
            recip(rstd[:], rstd[:], [t_rstd], [t_rstd])
            for kc in range(KC):
                xb, xbt = xblk.next()
                dma(xb[:], xT[kc * 128:(kc + 1) * 128, g * 512:(g + 1) * 512], [t_xT[g]], [xbt])
                stt(xb[:], xb[:], fnw[:, kc:kc + 1], rstd[:], ALU.mult, ALU.mult, [xbt, t_rstd, t_const], [xbt])
                ps, pt = psum.next()
                for j in range(4):
                    tr(ps[:, j * 128:(j + 1) * 128], xb[:, j * 128:(j + 1) * 128], [xbt], [pt])
                o, ot = tmpf.next()
                cp(o[:], ps[:], [pt], [ot], eng="act" if kc % 2 else "dve")
                dma(y_out[g * 512:(g + 1) * 512, kc * 128:(kc + 1) * 128].rearrange("(j t) f -> t j f", t=128),
                    o[:].rearrange("t (j f) -> t j f", j=4), [ot], [t_out])

    import os
    PH = os.environ.get("KPH", "afprtmg")
    for l in range(L):
        if "a" in PH:
            adaln(l)
        if "f" in PH:
            ffn(l, 0, 0)
        if "p" in PH:
            project(l)
        P.barrier()
        MIX = os.environ.get("KMIX", "012")
        if "r" in PH:
            recurrent_stub(l, tuple(m for m in (0, 1, 2) if str(m) not in MIX))
            if "0" in MIX:
                gla(l)
                post(l, 0, C_GLA_G, AF.Silu)
            if "1" in MIX:
                mlstm(l)
                post(l, 1, C_ML_O, AF.Sigmoid)
            if "2" in MIX:
                P.barrier()
                gdn_conv(l)
                P.barrier()
                gdn(l)
                P.barrier()
                post(l, 2, C_GD_G, AF.Silu)
        P.barrier()
        if "t" in PH:
            attention(l)
        P.barrier()
        if "m" in PH:
            merge_out(l)
        if "g" in PH:
            ffn(l, 1, 2)
    final()
    P.emit()
    st.close()
    return nc


def _rope_tables(TS, grid_w=64):
    rows = TS // grid_w
    row = np.repeat(np.arange(rows), grid_w).astype(np.float32)
    col = (np.arange(rows * grid_w) % grid_w).astype(np.float32)
    n_pairs = HD // 4
    inv = (10000.0 ** (-np.arange(n_pairs, dtype=np.float32) / n_pairs)).astype(np.float32)
    ang = np.concatenate([row[:, None] * inv, col[:, None] * inv], axis=-1)
    cos = np.repeat(np.cos(ang), 2, axis=1).T.astype(np.float32)
    sin = np.repeat(np.sin(ang), 2, axis=1).T.astype(np.float32)
    return np.ascontiguousarray(cos), np.ascontiguousarray(sin)


def _rmat():
    R = np.zeros((128, 128), np.float32)
    for i in range(64):
        R[2 * i + 1, 2 * i] = -1.0
        R[2 * i, 2 * i + 1] = 1.0
    return R


def make_in_maps(inp, L, TS, n_cores=8):
    f = lambda a: np.ascontiguousarray(np.asarray(a, dtype=np.float32))
    cos, sin = _rope_tables(TS)
    shared = dict(
        w_ada=f(inp["w_ada"][:L]), w_ffn_in=f(inp["w_ffn_in"][:L]), w_ffn_out=f(inp["w_ffn_out"][:L]),
        w_in=f(inp["w_in"][:L]), w_branch=f(inp["w_branch"][:L]), w_out=f(inp["w_out"][:L]),
        b_adaT=f(np.asarray(inp["b_ada"][:L]).reshape(L, 144, 128).transpose(0, 2, 1)),
        fnwT=f(np.asarray(inp["final_norm_w"]).reshape(KC, 128).T),
        qkw=f(np.stack([np.asarray(inp["q_norm_w"][:L]).T, np.asarray(inp["k_norm_w"][:L]).T], axis=-1)),
        rope_cos=cos, rope_sin=sin, rmat=_rmat(), ident=np.eye(128, dtype=np.float32),
        tri=np.stack([np.triu(np.ones((128, 128), np.float32)), np.tril(np.ones((128, 128), np.float32))]),
        gla_w2aug=f(np.concatenate([np.asarray(inp["gla_w2"][:L]), np.zeros((L, 2, 16, 512), np.float32),
                                    np.asarray(inp["gla_b2"][:L])[:, :, None, :]], axis=2)),
        gd_cwT=f(np.asarray(inp["gd_conv_w"][:L]).reshape(L, 5, 12, 128).transpose(0, 3, 2, 1)),
        gd_gb=f(np.broadcast_to(np.concatenate([np.asarray(inp["gd_dt_bias"][:L]), np.asarray(inp["gd_a_log"][:L])], axis=-1)[None],
                                (128, L, 2, 8))),
        strm=np.stack([np.triu(np.ones((128, 128), np.float32), 1), np.tril(np.ones((128, 128), np.float32), -1)]),
        ml_gb=f(np.broadcast_to(np.asarray(inp["ml_gate_b"][:L]).reshape(L, 2, 8)[None], (128, L, 2, 8))),
        normw=f(np.stack([np.asarray(inp["gla_norm_w"][:L]).T, np.asarray(inp["ml_norm_w"][:L]).T,
                          np.asarray(inp["gd_norm_w"][:L]).T], axis=-1)),
    )
    xp = np.asarray(inp["x_prompt"], np.float32)
    xs = np.asarray(inp["x_sample"], np.float32)
    maps = []
    for c in range(n_cores):
        b = min(c // 4, xs.shape[0] - 1)
        m = dict(shared)
        m["xin"] = np.ascontiguousarray(np.concatenate([xp[2 * c].reshape(256, D), xp[2 * c + 1].reshape(256, D),
                                                         xs[b, :TS]], axis=0))
        cc = np.stack([np.asarray(inp["c_ctx"], np.float32), np.asarray(inp["c"], np.float32)[b]], axis=-1)
        m["cT"] = np.ascontiguousarray(cc.reshape(KC, 128, 2).transpose(1, 0, 2))
        m["cache_k"] = f(inp["cache_k"][b, :L])
        m["cache_v"] = f(inp["cache_v"][b, :L])
        m["state_gla"] = f(inp["state_gla"][b, :L])
        m["state_mc"] = f(inp["state_mlstm_c"][b, :L])
        m["state_gd"] = f(inp["state_gdn"][b, :L])
        m["ml_n0T"] = f(np.asarray(inp["state_mlstm_n"][b, :L]).transpose(0, 1, 3, 2))
        m["ml_m0b"] = f(np.broadcast_to(np.asarray(inp["state_mlstm_m"][b, :L])[None], (128, L, 2, 4)))
        maps.append(m)
    return maps


_NC_CACHE = {}


def run_cfg(inp, L, TS, n_cores=8):
    key = (L, TS)
    if key not in _NC_CACHE:
        _NC_CACHE[key] = build(L, TS)
    nc = _NC_CACHE[key]
    maps = make_in_maps(inp, L, TS, n_cores)
    res = run_bass_kernel_spmd(nc, maps, core_ids=list(range(n_cores)))
    return res.results


def kernel(**inputs):
    L, TS = 4, 2048
    r = run_cfg(inputs, L, TS)
    y_prompt = np.stack([r[c]["y"][s * 256:(s + 1) * 256] for c in range(8) for s in range(2)], axis=0)
    y_sample = np.stack([r[0]["y"][512:], r[4]["y"][512:]], axis=0)
    cat = lambda k: np.concatenate([r[c][k] for c in range(8)], axis=0)
    return (y_prompt.astype(np.float32), y_sample.astype(np.float32), cat("nck"), cat("ncv"), cat("nsg"),
            cat("nsc"), cat("nsn"), cat("nsm"), cat("nsd"))
```
